# Optimizing a Trainium2 kernel written in Bass

```python
import jax, jax.numpy as jnp
from jax import lax
import numpy as np

D_MODEL = 1024
BATCH = 8
SEQ = 8192
DEPTH = 2

HEAD_DIM = 64
ATTN_WIDTH = D_MODEL // 2
N_ATTN_HEADS = ATTN_WIDTH // HEAD_DIM
N_KV_HEADS = 2
GQA_GROUP = N_ATTN_HEADS // N_KV_HEADS
KV_WIDTH = N_KV_HEADS * HEAD_DIM
N_BRANCH = 3
MLP_WIDTH = D_MODEL - ATTN_WIDTH
MLP_GROUP_DIM = 64
N_MLP_GROUPS = MLP_WIDTH // MLP_GROUP_DIM
MIX_WIDTH = ATTN_WIDTH + MLP_WIDTH
IN_SIZES = (ATTN_WIDTH, KV_WIDTH, KV_WIDTH, KV_WIDTH, KV_WIDTH, KV_WIDTH, KV_WIDTH,
            N_BRANCH * N_ATTN_HEADS, MLP_WIDTH, MLP_WIDTH)
IN_WIDTH = sum(IN_SIZES)
CMP_LEN = 32
CMP_STRIDE = 16
CMP_HIDDEN = 256
SLC_LEN = 64
SLC_TOPK = 16
WINDOW = 512
QBLK = 64
CHUNK = 128
D_FF = -(-8 * D_MODEL // (3 * 256)) * 256
ALPHA = (2.0 * DEPTH) ** 0.25
BETA = (8.0 * DEPTH) ** -0.25
LN_EPS = 1e-5
NEG = -1e30
FORCED_SCORE = 1e6

kernel_name = "hymba_nsa_gmlp_deepnorm_adaln"


def layer_norm(x, g, b):
    xf = x.astype(jnp.float32)
    mu = xf.mean(-1, keepdims=True)
    var = jnp.square(xf - mu).mean(-1, keepdims=True)
    return ((xf - mu) * lax.rsqrt(var + LN_EPS) * g + b).astype(x.dtype)


def rms_norm(x, g):
    xf = x.astype(jnp.float32)
    return (xf * lax.rsqrt(jnp.square(xf).mean(-1, keepdims=True) + LN_EPS) * g).astype(x.dtype)


def masked_softmax(s, mask):
    return jax.nn.softmax(jnp.where(mask, s.astype(jnp.float32), NEG), axis=-1)


def alibi_slopes():
    h = jnp.arange(1, N_ATTN_HEADS + 1, dtype=jnp.float32)
    return jnp.exp2(-8.0 * h / N_ATTN_HEADS)


def compress(tok, pos, w1, w2):
    B, S = tok.shape[:2]
    ch = tok.reshape(B, S // CMP_STRIDE, CMP_STRIDE, N_KV_HEADS, HEAD_DIM)
    blk = jnp.concatenate([ch[:, :-1], ch[:, 1:]], axis=2)
    blk = blk + pos[None, None, :, None, :]
    nc = blk.shape[1]
    blk = jnp.moveaxis(blk, 3, 2).reshape(B, nc, N_KV_HEADS, CMP_LEN * HEAD_DIM)
    return jax.nn.gelu(blk @ w1) @ w2


def nsa_mixer(q, k_cmp, v_cmp, k_slc, v_slc, k_win, v_win, gates):
    B, S = q.shape[:2]
    nc = k_cmp.shape[1]
    nslc = S // SLC_LEN
    topk = min(SLC_TOPK, nslc)
    nqb = S // QBLK
    q = (q * HEAD_DIM ** -0.5).reshape(B, S, N_KV_HEADS, GQA_GROUP, HEAD_DIM)
    gates = gates.reshape(B, S, N_KV_HEADS, GQA_GROUP, N_BRANCH)
    slopes = alibi_slopes().reshape(N_KV_HEADS, GQA_GROUP)
    cstart = CMP_STRIDE * jnp.arange(nc)
    cmp_end = cstart + CMP_LEN - 1
    sstart = SLC_LEN * jnp.arange(nslc)
    overlap = ((cstart[:, None] <= sstart[None, :] + SLC_LEN - 1)
               & (cmp_end[:, None] >= sstart[None, :])).astype(jnp.float32)
    ks_t = jnp.transpose(k_slc, (0, 2, 1, 3))
    vs_t = jnp.transpose(v_slc, (0, 2, 1, 3))
    pad = ((0, 0), (WINDOW, 0), (0, 0), (0, 0))
    kw_pad = jnp.pad(k_win, pad)
    vw_pad = jnp.pad(v_win, pad)
    jj = jnp.arange(nslc)
    n_sel = topk * SLC_LEN

    def block(i):
        t0 = i * QBLK
        t = t0 + jnp.arange(QBLK)
        qb = lax.dynamic_slice_in_dim(q, t0, QBLK, axis=1)
        gb = lax.dynamic_slice_in_dim(gates, t0, QBLK, axis=1)
        d_c = (t[:, None] - cmp_end[None, :]).astype(jnp.float32)
        ok_c = d_c >= 0
        s_c = jnp.einsum('bqkgd,bnkd->bkgqn', qb, k_cmp).astype(jnp.float32) \
            - slopes[:, :, None, None] * d_c
        p_c = masked_softmax(s_c, ok_c) * ok_c.any(-1)[:, None]
        o_c = jnp.einsum('bkgqn,bnkd->bqkgd', p_c.astype(v_cmp.dtype), v_cmp)
        imp = jnp.einsum('bkgqn,nj->bkqj', p_c, overlap)
        cur = t // SLC_LEN
        ok_j = sstart[None, :] <= t[:, None]
        forced = (jj[None, :] == 0) | (jj[None, :] == cur[:, None]) | (jj[None, :] == cur[:, None] - 1)
        score = jnp.where(ok_j & forced, FORCED_SCORE, jnp.where(ok_j, imp, NEG))
        _, sel = lax.top_k(score, topk)
        tok = (sel[..., None] * SLC_LEN + jnp.arange(SLC_LEN)).reshape(B, N_KV_HEADS, QBLK * n_sel)
        kg = jnp.take_along_axis(ks_t, tok[..., None], axis=2).reshape(B, N_KV_HEADS, QBLK, n_sel, HEAD_DIM)
        vg = jnp.take_along_axis(vs_t, tok[..., None], axis=2).reshape(B, N_KV_HEADS, QBLK, n_sel, HEAD_DIM)
        tok = tok.reshape(B, N_KV_HEADS, QBLK, n_sel)
        d_s = (t[None, None, :, None] - tok).astype(jnp.float32)[:, :, None]
        s_s = jnp.einsum('bqkgd,bkqld->bkgql', qb, kg).astype(jnp.float32) \
            - slopes[None, :, :, None, None] * d_s
        p_s = masked_softmax(s_s, d_s >= 0)
        o_s = jnp.einsum('bkgql,bkqld->bqkgd', p_s.astype(vg.dtype), vg)
        kwb = lax.dynamic_slice_in_dim(kw_pad, t0, QBLK + WINDOW, axis=1)
        vwb = lax.dynamic_slice_in_dim(vw_pad, t0, QBLK + WINDOW, axis=1)
        spos = t0 - WINDOW + jnp.arange(QBLK + WINDOW)
        d_w = t[:, None] - spos[None, :]
        ok_w = (d_w >= 0) & (d_w < WINDOW) & (spos[None, :] >= 0)
        s_w = jnp.einsum('bqkgd,bskd->bkgqs', qb, kwb).astype(jnp.float32) \
            - slopes[:, :, None, None] * d_w.astype(jnp.float32)
        p_w = masked_softmax(s_w, ok_w)
        o_w = jnp.einsum('bkgqs,bskd->bqkgd', p_w.astype(vwb.dtype), vwb)
        g = jax.nn.sigmoid(gb.astype(jnp.float32))
        o = g[..., 0:1] * o_c + g[..., 1:2] * o_s + g[..., 2:3] * o_w
        return o.astype(q.dtype)

    out = lax.map(block, jnp.arange(nqb))
    return jnp.moveaxis(out, 0, 1).reshape(B, S, ATTN_WIDTH)


def gmlp_mixer(u, v, vn_g, vn_b, w_s, b_s):
    B, S, _ = u.shape
    v = v.reshape(B, S, N_MLP_GROUPS, MLP_GROUP_DIM)
    v = layer_norm(v, vn_g.reshape(N_MLP_GROUPS, MLP_GROUP_DIM), vn_b.reshape(N_MLP_GROUPS, MLP_GROUP_DIM))
    v = v.reshape(B, S // CHUNK, CHUNK, N_MLP_GROUPS, MLP_GROUP_DIM)
    causal = jnp.tril(jnp.ones((CHUNK, CHUNK), dtype=bool))
    w = jnp.where(causal[None], w_s, 0)
    sv = jnp.einsum('gts,bnsgd->bntgd', w, v) + b_s.T[:, :, None]
    return u * sv.reshape(B, S, MLP_WIDTH)


def hybrid_layer(x, c_act, w_ada, b_ada, w_in, cmp_pos, cmp_w1, cmp_w2, vn_g, vn_b,
                 w_s, b_s, out_g, w_o, ln1_g, ln1_b, w1, w3, w2, ln2_g, ln2_b):
    B, S, _ = x.shape
    mod = (c_act @ w_ada + b_ada)[:, None, :]
    sh1, sc1, g1, sh2, sc2, g2 = jnp.split(mod, 6, axis=-1)
    h = x * (1 + sc1) + sh1
    proj = h @ w_in
    offs = np.cumsum(np.array(IN_SIZES))[:-1].tolist()
    q, kc, vc, ksl, vsl, kwn, vwn, gt, u, v = jnp.split(proj, offs, axis=-1)
    kvs = lambda a: a.reshape(B, S, N_KV_HEADS, HEAD_DIM)
    k_cmp = compress(kvs(kc), cmp_pos[0], cmp_w1[0], cmp_w2[0])
    v_cmp = compress(kvs(vc), cmp_pos[1], cmp_w1[1], cmp_w2[1])
    o_attn = nsa_mixer(q.reshape(B, S, N_ATTN_HEADS, HEAD_DIM), k_cmp, v_cmp,
                       kvs(ksl), kvs(vsl), kvs(kwn), kvs(vwn),
                       gt.reshape(B, S, N_ATTN_HEADS, N_BRANCH))
    o_mlp = gmlp_mixer(jax.nn.gelu(u), jax.nn.gelu(v), vn_g, vn_b, w_s, b_s)
    y = jnp.concatenate([o_attn, o_mlp], axis=-1).reshape(B, S, MIX_WIDTH // HEAD_DIM, HEAD_DIM)
    y = rms_norm(y, out_g.reshape(MIX_WIDTH // HEAD_DIM, HEAD_DIM)).reshape(B, S, MIX_WIDTH) @ w_o
    x = layer_norm(ALPHA * x + (1 + g1) * y, ln1_g, ln1_b)
    h = x * (1 + sc2) + sh2
    f = (jax.nn.silu(h @ w1) * (h @ w3)) @ w2
    return layer_norm(ALPHA * x + (1 + g2) * f, ln2_g, ln2_b)


def setup_inputs(seed: int = 0) -> dict:
    key = jax.random.key(seed)
    ks = jax.random.split(key, 21)
    L = DEPTH
    nrm = lambda k, shape, s: jax.random.normal(k, shape, jnp.float32) * s
    return {
        "x": nrm(ks[0], (BATCH, SEQ, D_MODEL), 1.0),
        "c": nrm(ks[1], (BATCH, D_MODEL), 1.0),
        "w_ada": nrm(ks[2], (L, D_MODEL, 6 * D_MODEL), 0.1 * D_MODEL ** -0.5),
        "b_ada": nrm(ks[3], (L, 6 * D_MODEL), 0.02),
        "w_in": nrm(ks[4], (L, D_MODEL, IN_WIDTH), D_MODEL ** -0.5),
        "cmp_pos": nrm(ks[5], (L, 2, CMP_LEN, HEAD_DIM), 0.02),
        "cmp_w1": nrm(ks[6], (L, 2, CMP_LEN * HEAD_DIM, CMP_HIDDEN), (CMP_LEN * HEAD_DIM) ** -0.5),
        "cmp_w2": nrm(ks[7], (L, 2, CMP_HIDDEN, HEAD_DIM), CMP_HIDDEN ** -0.5),
        "vn_g": 1.0 + nrm(ks[8], (L, MLP_WIDTH), 0.02),
        "vn_b": nrm(ks[9], (L, MLP_WIDTH), 0.02),
        "w_s": nrm(ks[10], (L, N_MLP_GROUPS, CHUNK, CHUNK), CHUNK ** -0.5),
        "b_s": 1.0 + nrm(ks[11], (L, N_MLP_GROUPS, CHUNK), 0.02),
        "out_g": 1.0 + nrm(ks[12], (L, MIX_WIDTH), 0.02),
        "w_o": nrm(ks[13], (L, MIX_WIDTH, D_MODEL), BETA * MIX_WIDTH ** -0.5),
        "ln1_g": 1.0 + nrm(ks[14], (L, D_MODEL), 0.02),
        "ln1_b": nrm(ks[15], (L, D_MODEL), 0.02),
        "w1": nrm(ks[16], (L, D_MODEL, D_FF), D_MODEL ** -0.5),
        "w3": nrm(ks[17], (L, D_MODEL, D_FF), D_MODEL ** -0.5),
        "w2": nrm(ks[18], (L, D_FF, D_MODEL), BETA * D_FF ** -0.5),
        "ln2_g": 1.0 + nrm(ks[19], (L, D_MODEL), 0.02),
        "ln2_b": nrm(ks[20], (L, D_MODEL), 0.02),
    }


def reference(x, c, w_ada, b_ada, w_in, cmp_pos, cmp_w1, cmp_w2, vn_g, vn_b, w_s, b_s,
              out_g, w_o, ln1_g, ln1_b, w1, w3, w2, ln2_g, ln2_b):
    c_act = jax.nn.silu(c)
    for l in range(DEPTH):
        x = hybrid_layer(x, c_act, w_ada[l], b_ada[l], w_in[l], cmp_pos[l], cmp_w1[l], cmp_w2[l],
                         vn_g[l], vn_b[l], w_s[l], b_s[l], out_g[l], w_o[l],
                         ln1_g[l], ln1_b[l], w1[l], w3[l], w2[l], ln2_g[l], ln2_b[l])
    return x
```

```python
import contextlib
import numpy as np
import concourse.bass as bass
import concourse.mybir as mybir
from concourse.bass_utils import run_bass_kernel_spmd

F32 = mybir.dt.float32
BF16 = mybir.dt.bfloat16
AF = mybir.ActivationFunctionType
ALU = mybir.AluOpType
AX = mybir.AxisListType

D = 1024
DEPTH = 2
SEQ = 8192
NB = 8
DFF = 2816
INW = 2328
ALPHA = (2.0 * DEPTH) ** 0.25
EPS = 1e-5
NEGM = -32768.0
C_Q, C_KC, C_VC, C_KS, C_VS, C_KW, C_VW, C_GT, C_U, C_V = 0, 512, 640, 768, 896, 1024, 1152, 1280, 1304, 1816
GC0 = 1.5957691216057308
GC1 = GC0 * 0.044715


class Sched:
    ENGS = ("tensor", "vector", "scalar", "gpsimd", "sync")
    EPOCH = 30000
    KD = 8

    def __init__(self, nc, stack, tag):
        self.nc = nc
        self.ops = {e: [] for e in self.ENGS}
        self.lastw = {}
        self.readers = {}
        self.tag = tag
        self.stack = stack
        self.dmas = {e: [] for e in self.ENGS}
        self.dma_sems = {}
        self.eng_sems = {}
        self.marks = []
        self.all_ops = []

    def _dep(self, o, d):
        if d is None or d is o:
            return
        if (not d["dma"]) and d["eng"] == "tensor" and o["eng"] == "tensor" and not o["dma"]:
            return
        o["deps"].append(d)

    def op(self, eng, fn, reads=(), writes=(), dma=False):
        o = dict(eng=eng, fn=fn, deps=[], dma=dma, flag=False, n=0, idx=len(self.ops[eng]), seq=len(self.all_ops), clock={})
        for r in reads:
            self._dep(o, self.lastw.get(r))
        for w in writes:
            self._dep(o, self.lastw.get(w))
            rd = self.readers.get(w)
            if rd:
                for d in rd["eng"].values():
                    self._dep(o, d)
                for d in rd["dma"]:
                    self._dep(o, d)
        for r in reads:
            rd = self.readers.setdefault(r, dict(eng={}, dma=[]))
            if dma:
                rd["dma"].append(o)
            else:
                rd["eng"][eng] = o
        for w in writes:
            self.lastw[w] = o
            self.readers[w] = dict(eng={}, dma=[])
        if dma:
            lst = self.dmas[eng]
            o["dq"] = len(lst)
            if len(lst) >= self.KD:
                o["deps"].append(lst[len(lst) - self.KD])
            lst.append(o)
        self.ops[eng].append(o)
        self.all_ops.append(o)
        return o

    def _key(self, d):
        if d["dma"]:
            return ("d", d["eng"], d["dq"] % self.KD), 16 * (d["dq"] // self.KD + 1)
        return ("e", d["eng"]), d["n"]

    def finalize(self):
        for o in self.all_ops:
            for d in o["deps"]:
                if not d["dma"]:
                    d["flag"] = True
        for e in self.ENGS:
            n = 0
            for o in self.ops[e]:
                if o["flag"] and not o["dma"]:
                    n += 1
                    o["n"] = n
            nep = max(1, -(-n // self.EPOCH))
            self.eng_sems[e] = [self.stack.enter_context(self.nc.semaphore(f"{self.tag}_{e}_{k}")) for k in range(nep)]
            if self.dmas[e]:
                self.dma_sems[e] = [self.stack.enter_context(self.nc.semaphore(f"{self.tag}_d{e}_{k}")) for k in range(self.KD)]
        know = {e: {} for e in self.ENGS}
        for o in self.all_ops:
            kn = know[o["eng"]]
            need = []
            for d in sorted(o["deps"], key=lambda d: -d["seq"]):
                key, val = self._key(d)
                if kn.get(key, 0) >= val:
                    continue
                need.append(d)
                for k2, v2 in d["clock"].items():
                    if kn.get(k2, 0) < v2:
                        kn[k2] = v2
                kn[key] = val
            o["waits"] = need
            if o["dma"] or o["flag"]:
                ck = dict(kn)
                if not o["dma"]:
                    ck[("e", o["eng"])] = max(ck.get(("e", o["eng"]), 0), o["n"])
                o["clock"] = ck

    def emit(self, ename, eng):
        def wait_for(d):
            if d["dma"]:
                val = 16 * (d["dq"] // self.KD + 1)
                eng.wait_ge(self.dma_sems[d["eng"]][d["dq"] % self.KD], val)
            else:
                ep = (d["n"] - 1) // self.EPOCH
                eng.wait_ge(self.eng_sems[d["eng"]][ep], (d["n"] - 1) % self.EPOCH + 1)

        for o in self.ops[ename]:
            for d in o["waits"]:
                wait_for(d)
            ins = o["fn"](eng)
            if o["dma"]:
                ins.then_inc(self.dma_sems[ename][o["dq"] % self.KD], 16)
            elif o["flag"]:
                ep = (o["n"] - 1) // self.EPOCH
                ins.then_inc(self.eng_sems[ename][ep], 1)
        for e in self.ENGS:
            for o in self.dmas[e][-self.KD:]:
                wait_for(o)

    def run(self):
        self.finalize()
        nc = self.nc
        with nc.Block() as block:
            @block.tensor
            def _(e):
                self.emit("tensor", e)

            @block.vector
            def _(e):
                self.emit("vector", e)

            @block.scalar
            def _(e):
                self.emit("scalar", e)

            @block.gpsimd
            def _(e):
                self.emit("gpsimd", e)

            @block.sync
            def _(e):
                self.emit("sync", e)


class RowBuf:
    def __init__(self, aps, rows):
        self.aps = aps
        self.rows = rows

    def __getitem__(self, key):
        rs, cs = key
        z = rs.start // self.rows
        assert (rs.stop - 1) // self.rows == z
        return self.aps[z][rs.start - z * self.rows:rs.stop - z * self.rows, cs]


def bcast(ap, pat):
    return bass.AP(ap.tensor, ap.offset, [list(ap.ap[0])] + [list(p) for p in pat])


class Ctx:
    pass


def sb(nc, stack, name, shape, dt):
    return stack.enter_context(nc.sbuf_tensor(name, list(shape), dt))


def mm(s, out, lhsT, rhs, start, stop, reads, writes):
    s.op("tensor", lambda e: e.matmul(out, lhsT, rhs, start=start, stop=stop), reads=reads, writes=writes)


def tr(s, out, in_, ident, reads, writes):
    s.op("tensor", lambda e: e.transpose(out, in_, ident), reads=reads, writes=writes)


def act(s, out, in_, func, reads, writes, bias=None, scale=None):
    kw = {}
    if bias is not None:
        kw["bias"] = bias
    if scale is not None:
        kw["scale"] = scale
    s.op("scalar", lambda e: e.activation(out, in_, func, **kw), reads=reads, writes=writes)


def vtt(s, out, in0, in1, op, reads, writes, eng="vector"):
    s.op(eng, lambda e: e.tensor_tensor(out, in0, in1, op), reads=reads, writes=writes)


def vts(s, out, in0, s1, s2, op0, op1, reads, writes, eng="vector"):
    if op1 is None:
        s.op(eng, lambda e: e.tensor_scalar(out, in0, s1, None, op0), reads=reads, writes=writes)
    else:
        s.op(eng, lambda e: e.tensor_scalar(out, in0, s1, s2, op0, op1), reads=reads, writes=writes)


def vstt(s, out, in0, sc, in1, op0, op1, reads, writes):
    s.op("vector", lambda e: e.scalar_tensor_tensor(out, in0, sc, in1, op0, op1), reads=reads, writes=writes)


def vcopy(s, out, in_, reads, writes, eng="vector"):
    s.op(eng, lambda e: e.tensor_copy(out, in_), reads=reads, writes=writes)


def dma(s, q, out, in_, reads, writes, **kw):
    s.op(q, lambda e: e.dma_start(out, in_, **kw), reads=reads, writes=writes, dma=True)


def gelu_tanh(s, C, out, x, tmp, shape_key, reads, writes):
    tk = shape_key
    vtt(s, tmp, x, x, ALU.mult, reads=reads, writes=[tk])
    vts(s, tmp, tmp, GC1, GC0, ALU.mult, ALU.add, reads=[tk], writes=[tk])
    vtt(s, tmp, tmp, x, ALU.mult, reads=[tk] + list(reads), writes=[tk])
    act(s, tmp, tmp, AF.Sigmoid, reads=[tk], writes=[tk])
    vtt(s, out, tmp, x, ALU.mult, reads=[tk] + list(reads), writes=writes)


def layer_norm_tile(s, C, r, out, gb, bb, key_r, key_out, tagk):
    st = C.ln_stats
    kst = ("lnst",)
    for h in range(2):
        s.op("vector", lambda e, h=h: e.bn_stats(st[:, h * 6:(h + 1) * 6], r[:, h * 512:(h + 1) * 512]),
             reads=[key_r], writes=[kst] if h == 0 else [kst])
    s.op("vector", lambda e: e.bn_aggr(C.ln_mv[:, 0:2], st[:, 0:12]), reads=[kst], writes=[("lnmv",)])
    act(s, C.ln_mv[:, 2:3], C.ln_mv[:, 1:2], AF.Sqrt, reads=[("lnmv",)], writes=[("lnrs",)], bias=C.eps_t[:, 0:1])
    s.op("vector", lambda e: e.reciprocal(C.ln_mv[:, 3:4], C.ln_mv[:, 2:3]), reads=[("lnrs",)], writes=[("lnri",)])
    vts(s, r, r, C.ln_mv[:, 0:1], C.ln_mv[:, 3:4], ALU.subtract, ALU.mult, reads=[key_r, ("lnmv",), ("lnri",)], writes=[key_r])
    vtt(s, r, r, gb, ALU.mult, reads=[key_r, tagk], writes=[key_r])
    vtt(s, out, r, bb, ALU.add, reads=[key_r, tagk], writes=[key_out])


def phase_ada(nc, G, S, L):
    with contextlib.ExitStack() as stack:
        s = Sched(nc, stack, "a")
        cT = sb(nc, stack, "a_cT", [128, 8], F32)
        cTs = sb(nc, stack, "a_cTs", [128, 8], F32)
        wa = [sb(nc, stack, f"a_wa{k}", [128, 8, 512], F32) for k in range(2)]
        brow = [sb(nc, stack, f"a_br{k}", [1, 512], F32) for k in range(2)]
        row = [sb(nc, stack, f"a_row{k}", [1, 512], F32) for k in range(2)]
        ones = sb(nc, stack, "a_ones", [1, 128], F32)
        gtile = [sb(nc, stack, f"a_gt{k}", [128, 512], F32) for k in range(2)]
        ps = [stack.enter_context(nc.psum_tensor(f"a_ps{k}", [128, 512], F32)) for k in range(4)]
        pT = stack.enter_context(nc.psum_tensor("a_pT", [128, 512], F32))
        dma(s, "sync", cT[:, :], G.c_d.rearrange("(c p) -> p c", p=128), reads=[], writes=["cT"],
            allow_slow_non_contiguous=True)
        act(s, cTs[:, :], cT[:, :], AF.Silu, reads=["cT"], writes=["cTs"])
        s.op("vector", lambda e: e.memset(ones[:, :], 1.0), writes=["ones"])
        it = 0
        for l in range(L):
            for j in range(12):
                k = it % 2
                it += 1
                dma(s, "sync", wa[k][:, :, :],
                    G.w_ada[l].rearrange("(c p) n -> p c n", p=128)[:, :, j * 512:(j + 1) * 512],
                    reads=[], writes=[("wa", k)])
                dma(s, "sync", brow[k][:, :], G.b_ada[l:l + 1, j * 512:(j + 1) * 512], reads=[], writes=[("br", k)])
                for c in range(8):
                    mm(s, ps[k][0:1, :], cTs[:, c:c + 1], wa[k][:, c, :], c == 0, c == 7,
                       reads=["cTs", ("wa", k)], writes=[("ps", k)])
                vtt(s, row[k][:, :], ps[k][0:1, :], brow[k][:, :], ALU.add, reads=[("ps", k), ("br", k)], writes=[("row", k)])
                for q in range(4):
                    col = l * 48 + j * 4 + q
                    mm(s, pT[:, col * 2:col * 2 + 2], row[k][0:1, q * 128:(q + 1) * 128], ones[0:1, 0:2], True, True,
                       reads=[("row", k), "ones"], writes=["pT"])
                if j in (4, 5, 10, 11):
                    mm(s, ps[2 + k][:, :], ones[0:1, 0:128], row[k][0:1, :], True, True,
                       reads=[("row", k), "ones"], writes=[("ps", 2 + k)])
                    vts(s, gtile[k][:, :], ps[2 + k][:, :], 1.0, None, ALU.add, None, reads=[("ps", 2 + k)], writes=[("gt", k)])
                    which = 0 if j < 6 else 1
                    half = j % 2
                    dma(s, "sync", G.gb_d[l, which, :, half * 512:(half + 1) * 512], gtile[k][:, :],
                        reads=[("gt", k)], writes=["gb_d"])
        vcopy(s, G.modT[:, 0:L * 48], bcast(pT[:, 0:1], [[2, L * 48]]), reads=["pT"], writes=["modT"])
        for l in range(L):
            for j in (1, 4):
                a0 = l * 48 + j * 8
                vts(s, G.modT[:, a0:a0 + 8], G.modT[:, a0:a0 + 8], 1.0, None, ALU.add, None, reads=["modT"], writes=["modT"])
        s.run()


def phase_ffn(nc, G, S, l, x_in, x_out):
    ST = 256
    NSUB = ST // 128
    NST = S // ST
    with contextlib.ExitStack() as stack:
        s = Sched(nc, stack, f"f{l}")
        C = Ctx()
        w1b = sb(nc, stack, f"f{l}_w1", [128, 8, DFF], BF16)
        w3b = sb(nc, stack, f"f{l}_w3", [128, 8, DFF], BF16)
        w2b = sb(nc, stack, f"f{l}_w2", [128, 22, D], BF16)
        lng = sb(nc, stack, f"f{l}_lng", [128, D], F32)
        lnb = sb(nc, stack, f"f{l}_lnb", [128, D], F32)
        g2b = sb(nc, stack, f"f{l}_g2b", [128, D], F32)
        xt = [sb(nc, stack, f"f{l}_x{k}", [128, NSUB, D], F32) for k in range(2)]
        hT = sb(nc, stack, f"f{l}_hT", [128, 8, ST], BF16)
        gT = sb(nc, stack, f"f{l}_gT", [128, 22, ST], BF16)
        sa = [sb(nc, stack, f"f{l}_sa{k}", [128, ST], F32) for k in range(2)]
        rt = [sb(nc, stack, f"f{l}_r{k}", [128, D], F32) for k in range(2)]
        ot = [sb(nc, stack, f"f{l}_o{k}", [128, D], F32) for k in range(2)]
        ident = sb(nc, stack, f"f{l}_id", [128, 128], F32)
        C.ln_stats = sb(nc, stack, f"f{l}_lst", [128, 12], F32)
        C.ln_mv = sb(nc, stack, f"f{l}_lmv", [128, 4], F32)
        C.eps_t = sb(nc, stack, f"f{l}_eps", [128, 1], F32)
        ps = [stack.enter_context(nc.psum_tensor(f"f{l}_ps{k}", [128, 512], F32)) for k in range(8)]
        scT = G.modT[:, l * 48 + 32:l * 48 + 40]
        shT = G.modT[:, l * 48 + 24:l * 48 + 32]

        s.op("vector", lambda e: e.memset(C.eps_t[:, :], EPS), writes=["eps"])
        dma(s, "sync", ident[:, :], G.c_ident[:, :], reads=[], writes=["ident"])
        dma(s, "sync", lng[:, :], G.ln2_g[l:l + 1, :].partition_broadcast(128), reads=[], writes=["lnp"])
        dma(s, "sync", lnb[:, :], G.ln2_b[l:l + 1, :].partition_broadcast(128), reads=[], writes=["lnp"])
        dma(s, "sync", g2b[:, :], G.gb_d[l, 1, :, :], reads=[], writes=["g2b"])
        for c in range(8):
            for h in range(2):
                dma(s, "gpsimd", w1b[:, c, h * 1408:(h + 1) * 1408], G.w1[l, c * 128:(c + 1) * 128, h * 1408:(h + 1) * 1408],
                    reads=[], writes=["w1"])
                dma(s, "gpsimd", w3b[:, c, h * 1408:(h + 1) * 1408], G.w3[l, c * 128:(c + 1) * 128, h * 1408:(h + 1) * 1408],
                    reads=[], writes=["w3"])
        for f in range(22):
            dma(s, "gpsimd", w2b[:, f, :], G.w2[l, f * 128:(f + 1) * 128, :], reads=[], writes=["w2"])

        pi = [0]

        def nps():
            pi[0] = (pi[0] + 1) % 8
            return pi[0]

        dma(s, "sync", xt[0][:, :, :], x_in[0:ST, :].rearrange("(a p) d -> p a d", p=128), reads=[], writes=[("x", 0)])
        for st in range(NST):
            k = st % 2
            T0 = st * ST
            if st + 1 < NST:
                dma(s, "sync", xt[1 - k][:, :, :], x_in[T0 + ST:T0 + 2 * ST, :].rearrange("(a p) d -> p a d", p=128),
                    reads=[], writes=[("x", 1 - k)])
            for a in range(NSUB):
                for c4 in range(2):
                    b = nps()
                    for cc in range(4):
                        c = c4 * 4 + cc
                        tr(s, ps[b][:, cc * 128:(cc + 1) * 128], xt[k][:, a, c * 128:(c + 1) * 128], ident[:, :],
                           reads=[("x", k), "ident"], writes=[("ps", b)])
                    for cc in range(4):
                        c = c4 * 4 + cc
                        act(s, hT[:, c, a * 128:(a + 1) * 128], ps[b][:, cc * 128:(cc + 1) * 128], AF.Identity,
                            reads=[("ps", b), "modT"], writes=["hT"], bias=shT[:, c:c + 1], scale=scT[:, c:c + 1])
            for f in range(22):
                ba = nps()
                for c in range(8):
                    mm(s, ps[ba][:, 0:ST], w1b[:, c, f * 128:(f + 1) * 128], hT[:, c, :], c == 0, c == 7,
                       reads=["w1", "hT"], writes=[("ps", ba)])
                bb = nps()
                for c in range(8):
                    mm(s, ps[bb][:, 0:ST], w3b[:, c, f * 128:(f + 1) * 128], hT[:, c, :], c == 0, c == 7,
                       reads=["w3", "hT"], writes=[("ps", bb)])
                kk = f % 2
                act(s, sa[kk][:, :], ps[ba][:, 0:ST], AF.Silu, reads=[("ps", ba)], writes=[("sa", kk)])
                vtt(s, gT[:, f, :], sa[kk][:, :], ps[bb][:, 0:ST], ALU.mult, reads=[("sa", kk), ("ps", bb)], writes=["gT"])
            for a in range(NSUB):
                kr = (st * NSUB + a) % 2
                for h in range(2):
                    b = nps()
                    for f in range(22):
                        mm(s, ps[b][:, :], gT[:, f, a * 128:(a + 1) * 128], w2b[:, f, h * 512:(h + 1) * 512], f == 0, f == 21,
                           reads=["gT", "w2"], writes=[("ps", b)])
                    vtt(s, rt[kr][:, h * 512:(h + 1) * 512], ps[b][:, :], g2b[:, h * 512:(h + 1) * 512], ALU.mult,
                        reads=[("ps", b), "g2b"], writes=[("r", kr)])
                vstt(s, rt[kr][:, :], xt[k][:, a, :], ALPHA, rt[kr][:, :], ALU.mult, ALU.add,
                     reads=[("x", k), ("r", kr)], writes=[("r", kr)])
                layer_norm_tile(s, C, rt[kr][:, :], ot[kr][:, :], lng[:, :], lnb[:, :], ("r", kr), ("o", kr), "lnp")
                t0 = T0 + a * 128
                dma(s, "sync", x_out[t0:t0 + 128, :], ot[kr][:, :], reads=[("o", kr)], writes=["xout"])
        s.run()


W_NAMES = ["w_ada", "b_ada", "w_in", "cmp_pos", "cmp_w1", "cmp_w2", "vn_g", "vn_b", "w_s", "b_s",
           "out_g", "w_o", "ln1_g", "ln1_b", "w1", "w3", "w2", "ln2_g", "ln2_b"]


def build(S, L, shapes, mode="full"):
    nc = bass.Bass("TRN2", target_bir_lowering=False)
    G = Ctx()
    G.x_d = nc.dram_tensor("x", [S, D], F32, kind="ExternalInput").ap()
    G.c_d = nc.dram_tensor("c", [D], F32, kind="ExternalInput").ap()
    for n in W_NAMES:
        setattr(G, n, nc.dram_tensor(n, list(shapes[n]), F32, kind="ExternalInput").ap())
    for n, shp in const_shapes(S).items():
        setattr(G, n, nc.dram_tensor(n, list(shp), F32, kind="ExternalInput").ap())
    G.out_d = nc.dram_tensor("out", [S, D], F32, kind="ExternalOutput").ap()
    G.gb_d = nc.dram_tensor("gb_d", [L, 2, 128, D], F32, kind="Internal").ap()
    G.xs = [RowBuf([nc.dram_tensor(f"xs{k}_{z}", [min(S, 2048), D], F32, kind="Internal").ap()
                    for z in range(max(1, S // 2048))], 2048) for k in range(2)]
    with contextlib.ExitStack() as top:
        G.modT = sb(nc, top, "modT", [128, L * 48], F32)
        phase_ada(nc, G, S, L)
        nc.all_engine_barrier()
        if mode == "ffn":
            phase_ffn(nc, G, S, 0, RowBuf([G.x_d], S), RowBuf([G.out_d], S))
        elif mode == "mix":
            phase_mix(nc, G, S, 0, RowBuf([G.x_d], S), RowBuf([G.out_d], S))
        else:
            cur = RowBuf([G.x_d], S)
            for l in range(L):
                phase_mix(nc, G, S, l, cur, G.xs[0])
                nc.all_engine_barrier()
                dst = RowBuf([G.out_d], S) if l == L - 1 else G.xs[1]
                phase_ffn(nc, G, S, l, G.xs[0], dst)
                nc.all_engine_barrier()
                cur = dst
    return nc


def const_shapes(S):
    NCP = S // 16
    return {"c_ident": (128, 128), "c_kaug": (3, S), "c_caug": (3, NCP), "c_qaug": (3, 2, 4, S),
            "c_ovl": (128, max(1, NCP // 128), 128), "c_mcmp": (128, 2176), "c_mdiag": (128, 128),
            "c_mtail": (128, 128), "c_tril": (128, 128), "c_esel": (128, 4096)}


def make_consts(S):
    NCP = S // 16
    NCH = max(1, NCP // 128)
    t = np.arange(S)
    kaug = np.stack([t // 128, t % 128, np.ones(S)]).astype(np.float32)
    pos = 16 * np.arange(NCP) + 31
    caug = np.stack([pos // 128, pos % 128, np.ones(NCP)]).astype(np.float32)
    qaug = np.zeros((3, 2, 4, S), np.float32)
    for kvh in range(2):
        for g in range(4):
            sl = 2.0 ** (-(kvh * 4 + g + 1))
            qaug[0, kvh, g] = 128 * sl
            qaug[1, kvh, g] = sl
            qaug[2, kvh, g] = -sl * (128 * (t // 128) + 64)
    n = np.arange(NCH * 128)
    j = np.arange(128)
    ov = ((16 * n[:, None] <= 64 * j[None, :] + 63) & (16 * n[:, None] + 31 >= 64 * j[None, :]) & (n[:, None] < NCP - 1))
    ovl = ov.astype(np.float32).reshape(NCH, 128, 128).transpose(1, 0, 2).copy()
    p = np.arange(128)
    u = np.arange(2176)
    mcmp = np.where(u[None, :] >= 16 * p[:, None] + 31, 0.0, NEGM).astype(np.float32)
    mdiag = np.where(p[:, None] <= p[None, :], 0.0, NEGM).astype(np.float32)
    mtail = np.where(p[:, None] > p[None, :], 0.0, NEGM).astype(np.float32)
    tril = (p[:, None] <= p[None, :]).astype(np.float32)
    w = np.arange(4096)
    esel = ((p[:, None] % 64) == (w[None, :] // 64)).astype(np.float32)
    return {"c_esel": esel, "c_ident": np.eye(128, dtype=np.float32), "c_kaug": kaug, "c_caug": caug, "c_qaug": qaug,
            "c_ovl": ovl, "c_mcmp": mcmp, "c_mdiag": mdiag, "c_mtail": mtail, "c_tril": tril}


def phase_mix(nc, G, S, l, x_in, x_out):
    ST = 128
    NSUB = ST // 128
    NB_ = ST // 16
    NST = S // ST
    NT = S // 128
    NCP = S // 16
    NCH = max(1, NCP // 128)
    NSLC = S // 64
    TOPK = min(16, NSLC)
    with contextlib.ExitStack() as stack:
        s = Sched(nc, stack, f"m{l}")
        C = Ctx()
        T = lambda name, shape, dt: sb(nc, stack, f"m{l}_" + name, shape, dt)
        winb = T("win", [128, 8, INW], BF16)
        wob = T("wo", [128, 8, D], BF16)
        w1c = T("w1c", [128, 2, 16, 256], BF16)
        w2c = T("w2c", [128, 2, 2, 64], BF16)
        posTf = T("posTf", [128, 2, 16], F32)
        posT = T("posT", [128, 2, 16], BF16)
        posb = T("posb", [128, 4], F32)
        wsT = T("wsT", [128, 8, 128], BF16)
        bsT = T("bsT", [128, 8], F32)
        kTs = T("kTs", [67, 2, S], BF16)
        vs = T("vs", [128, NT, 2, 65], BF16)
        kTc = T("kTc", [67, 2, NCP], BF16)
        vcT = T("vcT", [64, 2, NCP], BF16)
        vc = T("vc", [128, NCH, 2, 65], BF16)
        kTw = T("kTw", [67, 2, 1024], BF16)
        vw = T("vw", [128, 8, 2, 65], BF16)
        identf = T("idf", [128, 128], F32)
        identb = T("idb", [128, 128], BF16)
        mcmp = T("mcmp", [128, 2176], BF16)
        mdiag = T("mdiag", [128, 128], BF16)
        mtail = T("mtail", [128, 128], BF16)
        tril = T("tril", [128, 128], F32)
        esel = T("esel", [128, 4096], BF16)
        ovl = T("ovl", [128, NCH, 128], BF16)
        ln1g = T("ln1g", [128, D], F32)
        ln1b = T("ln1b", [128, D], F32)
        vng = T("vng", [128, 512], F32)
        vnb = T("vnb", [128, 512], F32)
        outgT = T("outgT", [128, 8], F32)
        xt = [T(f"xt{k}", [128, NSUB, D], F32) for k in range(3)]
        hT = T("hT", [128, 8, ST], BF16)
        qT = [T(f"qT{k}", [67, 2, 4, ST], BF16) for k in range(2)]
        kcb = T("kcb", [128, 2, 2, ST + 16], BF16)
        hidT = T("hidT", [128, 256], BF16)
        pT = [T(f"pT{k}", [128, 512], BF16) for k in range(4)]
        nmq = T("nmq", [128, 128], BF16)
        nmT = T("nmT", [128, 2, 128], BF16)
        impsb = T("impsb", [128, 128], F32)
        imptmp = T("imptmp", [128, 128], F32)
        m8 = T("m8", [128, 16], F32)
        rc = T("rc", [128, 8], F32)
        den = T("den", [128, 24], F32)
        coef = T("coef", [128, 24], F32)
        oc_sb = T("ocsb", [128, 2, 260], F32)
        ctmp = T("ctmp", [128, 4, 64], F32)
        u_sb = T("u", [128, 512], F32)
        hidf = T("hidf", [128, 256], F32)
        hidt = T("hidt", [128, 256], F32)
        v_f = T("vf", [128, 8, 64], F32)
        t512 = T("t512", [128, 512], F32)
        v_bf = T("vbf", [128, 8, 64], BF16)
        st8 = T("st8", [128, 40], F32)
        gsig = [T(f"gsig{k}", [128, 24], F32) for k in range(2)]
        yraw = T("yraw", [128, 16, 64], F32)
        ss16 = T("ss16", [128, 32], F32)
        yT = T("yT", [128, 8, 128], BF16)
        rt = T("rt", [128, D], F32)
        C.ln_stats = T("lst", [128, 12], F32)
        C.ln_mv = T("lmv", [128, 4], F32)
        C.eps_t = T("eps", [128, 1], F32)
        ps = [stack.enter_context(nc.psum_tensor(f"m{l}_ps{k}", [128, 512], F32)) for k in range(8)]
        y_bf = rt.bitcast(BF16)
        pend = []

        def flush():
            while pend:
                pend.pop(0)()
        psb = [p.bitcast(BF16) for p in ps]
        scT = G.modT[:, l * 48 + 8:l * 48 + 16]
        shT = G.modT[:, l * 48 + 0:l * 48 + 8]
        PS_OC, PS_OS, PS_OW, PS_IMP = 2, 3, 4, 5

        gi = [0]

        def gbank():
            return 7

        si = [0]

        def sbank():
            si[0] = (si[0] + 1) % 3
            return (0, 1, 6)[si[0]]

        pi = [0]

        def nextp():
            pi[0] = (pi[0] + 1) % 4
            return pi[0]

        V = "vector"
        s.op(V, lambda e: e.memset(C.eps_t[:, :], EPS), writes=["eps"])
        s.op(V, lambda e: e.memset(kTc[0:64, :, :], 0.0), writes=["kTc"])
        s.op(V, lambda e: e.memset(vcT[:, :, :], 0.0), writes=["vcT"])
        s.op(V, lambda e: e.memset(vc[:, :, :, :], 0.0), writes=["vc"])
        s.op(V, lambda e: e.memset(vc[:, :, :, 64:65], 1.0), writes=["vc"])
        s.op(V, lambda e: e.memset(vs[:, :, :, 64:65], 1.0), writes=["vs1"])
        s.op(V, lambda e: e.memset(vw[:, :, :, 64:65], 1.0), writes=["vw1"])
        s.op(V, lambda e: e.memset(nmq[:, :], 0.0), writes=["nmq"])
        s.op(V, lambda e: e.memset(kcb[:, :, :, :], 0.0), writes=["kcb"])
        dma(s, "sync", identf[:, :], G.c_ident[:, :], [], ["identf"])
        dma(s, "sync", tril[:, :], G.c_tril[:, :], [], ["tril"])
        dma(s, "gpsimd", identb[:, :], G.c_ident[:, :], [], ["identb"])
        for z in range(2):
            dma(s, "gpsimd", mcmp[:, z * 1088:(z + 1) * 1088], G.c_mcmp[:, z * 1088:(z + 1) * 1088], [], ["mcmp"])
        dma(s, "gpsimd", mdiag[:, :], G.c_mdiag[:, :], [], ["mdiag"])
        dma(s, "gpsimd", mtail[:, :], G.c_mtail[:, :], [], ["mtail"])
        for h_ in range(2):
            for z in range(2):
                o_ = h_ * 2048 + z * 1024
                dma(s, "gpsimd", esel[:, o_:o_ + 1024], G.c_esel[:, o_:o_ + 1024], [], ["esel"])
        dma(s, "gpsimd", ovl[:, :, :], G.c_ovl[:, :, :], [], ["ovl"])
        for kvh in range(2):
            for z in range(0, S, 1024):
                dma(s, "gpsimd", kTs[64:67, kvh, z:z + 1024], G.c_kaug[:, z:z + 1024], [], ["kTs_aug"])
            dma(s, "gpsimd", kTc[64:67, kvh, :], G.c_caug[:, :], [], ["kTc_aug"])
        dma(s, "sync", ln1g[:, :], G.ln1_g[l:l + 1, :].partition_broadcast(128), [], ["lnp"])
        dma(s, "sync", ln1b[:, :], G.ln1_b[l:l + 1, :].partition_broadcast(128), [], ["lnp"])
        dma(s, "sync", vng[:, :], G.vn_g[l:l + 1, :].partition_broadcast(128), [], ["vnp"])
        dma(s, "sync", vnb[:, :], G.vn_b[l:l + 1, :].partition_broadcast(128), [], ["vnp"])
        dma(s, "sync", outgT[:, :], G.out_g[l].rearrange("(c p) -> p c", p=128), [], ["outgT"], allow_slow_non_contiguous=True)
        dma(s, "sync", bsT[:, :], G.b_s[l].rearrange("g t -> t g"), [], ["bsT"], allow_slow_non_contiguous=True)
        dma(s, "sync", bcast(rt[:, 0:1], [[128, 8], [1, 128]]), G.w_s[l].rearrange("g t s -> t g s"), [], ["rt"])
        for kv in range(2):
            dma(s, "sync", posTf[:, kv, :], G.cmp_pos[l, kv].rearrange("r d -> (r d)").rearrange("(r p) -> p r", p=128),
                [], ["posTf"], allow_slow_non_contiguous=True)
            dma(s, "gpsimd", w1c[:, kv, :, :], G.cmp_w1[l, kv].rearrange("(r p) h -> p r h", p=128), [], ["w1c"])
            dma(s, "gpsimd", w2c[:, kv, :, :], G.cmp_w2[l, kv].rearrange("(h p) d -> p h d", p=128), [], ["w2c"])
        for c in range(8):
            for h in range(2):
                dma(s, "gpsimd", winb[:, c, h * 1164:(h + 1) * 1164], G.w_in[l, c * 128:(c + 1) * 128, h * 1164:(h + 1) * 1164],
                    [], ["win"])
            dma(s, "gpsimd", wob[:, c, :], G.w_o[l, c * 128:(c + 1) * 128, :], [], ["wo"])
        vcopy(s, posT[:, :, :], posTf[:, :, :], ["posTf"], ["posT"])
        for g in range(8):
            b = gbank()
            tr(s, ps[b][:, 0:128], rt[:, g * 128:(g + 1) * 128], identf[:, :], ["rt", "identf"], [("ps", b)])
            vtt(s, wsT[:, g, :], ps[b][:, 0:128], tril[:, :], ALU.mult, [("ps", b), "tril"], ["wsT"])
        dma(s, "sync", rt[:, :], G.gb_d[l, 0, :, :], [], ["rt"])
        for c in range(8):
            vtt(s, wob[:, c, :], wob[:, c, :], rt[:, :], ALU.mult, ["wo", "rt"], ["wo"])
        b = gbank()
        for kv in range(2):
            for half in range(2):
                cidx = kv * 2 + half
                for r in range(16):
                    mm(s, ps[b][:, cidx * 2:cidx * 2 + 1], w1c[:, kv, r, half * 128:(half + 1) * 128], posT[:, kv, r:r + 1],
                       r == 0, r == 15, ["w1c", "posT"], [("ps", b)])
        vcopy(s, posb[:, :], bcast(ps[b][:, 0:1], [[2, 4]]), [("ps", b)], ["posb"])

        def exp_tile(sbk, reads_extra=()):
            k = nextp()
            act(s, pT[k][:, :], ps[sbk][:, :], AF.Exp, [("ps", sbk)], [("pT", k)])
            return k

        def s3(b):
            return bcast(ps[b][:, 0:1], [[128, 4], [1, 128]])

        later = []

        def pair(s_stage, pv_stage, npop=0):
            k = s_stage()
            while len(pend) >= 2:
                pend.pop(0)()
            pend.append(lambda: pv_stage(k))
            for _ in range(npop):
                if later:
                    later.pop(0)()

        def flush_later():
            while later:
                later.pop(0)()

        def make_R(i, xk):
            pieces = []

            def p_rms():
                rt3 = bcast(rt[:, 0:1], [[64, 16], [1, 64]])
                vtt(s, rt3, yraw[:, :, :], yraw[:, :, :], ALU.mult, ["yraw"], ["rt"])
                s.op(V, lambda e: e.tensor_reduce(ss16[:, 0:16], rt3, AX.X, ALU.add), reads=["rt"], writes=["ss16"])
                act(s, ss16[:, 16:32], ss16[:, 0:16], AF.Sqrt, ["ss16", "eps"], ["ss16b"], bias=C.eps_t[:, 0:1], scale=1.0 / 64)
                s.op(V, lambda e: e.reciprocal(ss16[:, 0:16], ss16[:, 16:32]), reads=["ss16b"], writes=["ss16"])
                vtt(s, bcast(y_bf[:, 0:1], [[64, 16], [1, 64]]), yraw[:, :, :], bcast(ss16[:, 0:1], [[1, 16], [0, 64]]), ALU.mult,
                    ["yraw", "ss16"], ["rt"])
            pieces.append(p_rms)

            def p_tr(c4):
                b = gbank()
                for cc in range(4):
                    c = c4 * 4 + cc
                    tr(s, psb[b][:, cc * 128:(cc + 1) * 128], y_bf[:, c * 128:(c + 1) * 128], identb[:, :],
                       ["rt", "identb"], [("ps", b)])
                for cc in range(4):
                    c = c4 * 4 + cc
                    act(s, yT[:, c, :], psb[b][:, cc * 128:(cc + 1) * 128], AF.Copy, [("ps", b), "outgT"], ["yT"],
                        scale=outgT[:, c:c + 1])
            pieces.append(lambda: p_tr(0))
            pieces.append(lambda: p_tr(1))

            def p_out(h):
                b = gbank()
                for c in range(8):
                    mm(s, ps[b][:, :], yT[:, c, :], wob[:, c, h * 512:(h + 1) * 512], c == 0, c == 7, ["yT", "wo"], [("ps", b)])
                vstt(s, rt[:, h * 512:(h + 1) * 512], xt[xk][:, 0, h * 512:(h + 1) * 512], ALPHA, ps[b][:, :], ALU.mult, ALU.add,
                     [("x", xk), ("ps", b)], ["rt"])
            pieces.append(lambda: p_out(0))
            pieces.append(lambda: p_out(1))

            def p_ln():
                layer_norm_tile(s, C, rt[:, :], rt[:, :], ln1g[:, :], ln1b[:, :], "rt", "rt", "lnp")
                dma(s, "sync", x_out[i * 128:(i + 1) * 128, :], rt[:, :], ["rt"], ["xout"])
            pieces.append(p_ln)
            return pieces

        t5 = bcast(t512[:, 0:1], [[64, 8], [1, 64]])

        def make_P(i):
            T0 = i * 128
            xk = i % 3
            slot = i % 8
            qTi = qT[i % 2]
            gs = gsig[i % 2]
            nlo = max(0, NB_ * i - 1)
            nhi = NB_ * i + NB_ - 1
            cnt = nhi - nlo
            c0 = 16 * nlo - T0 + 16
            lhs_tok = lambda c: hT[:, c, :]
            st_ = {}
            pieces = []

            def p_x(c4):
                if c4 == 0:
                    dma(s, "gpsimd", qTi[64:67, :, :, :], G.c_qaug[:, :, :, T0:T0 + ST], [], [("qT_aug", i % 2)])
                    for kvh in range(2):
                        dma(s, "gpsimd", kTw[64:67, kvh, slot * 128:slot * 128 + 128], G.c_kaug[:, T0:T0 + ST], [], [("kTw_aug", slot)])
                b = gbank()
                for cc in range(4):
                    c = c4 * 4 + cc
                    tr(s, ps[b][:, cc * 128:(cc + 1) * 128], xt[xk][:, 0, c * 128:(c + 1) * 128], identf[:, :],
                       [("x", xk), "identf"], [("ps", b)])
                for cc in range(4):
                    c = c4 * 4 + cc
                    act(s, hT[:, c, :], ps[b][:, cc * 128:(cc + 1) * 128], AF.Identity,
                        [("ps", b), "modT"], ["hT"], bias=shT[:, c:c + 1], scale=scT[:, c:c + 1])
            pieces.append(lambda: p_x(0))
            pieces.append(lambda: p_x(1))

            def p_kcvc(kvh):
                for kv in range(2):
                    b = gbank()
                    col0 = (C_KC if kv == 0 else C_VC) + 64 * kvh
                    for half in range(2):
                        for c in range(8):
                            mm(s, ps[b][half * 64:(half + 1) * 64, 0:ST], winb[:, c, col0:col0 + 64], hT[:, c, :], c == 0, c == 7,
                               ["win", "hT"], [("ps", b)])
                    vcopy(s, kcb[0:64, kv, kvh, 16:16 + ST], ps[b][0:64, 0:ST], [("ps", b)], ["kcb"])
                    act(s, kcb[64:128, kv, kvh, 15:15 + ST], ps[b][64:128, 0:ST], AF.Copy, [("ps", b)], ["kcb"])
            pieces.append(lambda: p_kcvc(0))
            pieces.append(lambda: p_kcvc(1))

            def p_c1(kv):
                if kv == 0:
                    st_["bh"] = gbank()
                bh = st_["bh"]
                for kvh in range(2):
                    for half in range(2):
                        combo = kv * 4 + kvh * 2 + half
                        for r in range(16):
                            rhs = bcast(kcb[:, kv, kvh, c0 + 2 * r:c0 + 2 * r + 1], [[16, cnt]])
                            mm(s, ps[bh][:, combo * 32:combo * 32 + cnt], w1c[:, kv, r, half * 128:(half + 1) * 128], rhs,
                               r == 0, r == 15, ["w1c", "kcb"], [("ps", bh)])
            pieces.append(lambda: p_c1(0))
            pieces.append(lambda: p_c1(1))

            def p_c1b():
                bh = st_["bh"]
                for kv in range(2):
                    for kvh in range(2):
                        for half in range(2):
                            combo = kv * 4 + kvh * 2 + half
                            vts(s, hidf[:, combo * 32:combo * 32 + 32], ps[bh][:, combo * 32:combo * 32 + 32],
                                posb[:, kv * 2 + half:kv * 2 + half + 1], None, ALU.add, None, [("ps", bh), "posb"], ["hidf"])
                gelu_tanh(s, C, hidT[:, :], hidf[:, :], hidt[:, :], "hidt", ["hidf"], ["hidT"])
                vcopy(s, kcb[:, :, :, 0:16], kcb[:, :, :, ST:ST + 16], ["kcb"], ["kcb"])
            pieces.append(p_c1b)

            def p_q(h0):
                for h in range(h0, h0 + 4):
                    b = gbank()
                    for c in range(8):
                        mm(s, ps[b][0:64, 0:ST], winb[:, c, C_Q + 64 * h:C_Q + 64 * h + 64], hT[:, c, :], c == 0, c == 7,
                           ["win", "hT"], [("ps", b)])
                    act(s, qTi[0:64, h // 4, h % 4, :], ps[b][0:64, 0:ST], AF.Identity, [("ps", b)], [("qT", i % 2)], scale=0.125)
            pieces.append(lambda: p_q(0))
            pieces.append(lambda: p_q(4))

            def p_k(kvh):
                b = gbank()
                for c in range(8):
                    mm(s, ps[b][0:64, 0:ST], winb[:, c, C_KS + 64 * kvh:C_KS + 64 * kvh + 64], hT[:, c, :], c == 0, c == 7,
                       ["win", "hT"], [("ps", b)])
                vcopy(s, kTs[0:64, kvh, T0:T0 + ST], ps[b][0:64, 0:ST], [("ps", b)], [("kTs", i)])
                b = gbank()
                for c in range(8):
                    mm(s, ps[b][0:64, 0:ST], winb[:, c, C_KW + 64 * kvh:C_KW + 64 * kvh + 64], hT[:, c, :], c == 0, c == 7,
                       ["win", "hT"], [("ps", b)])
                vcopy(s, kTw[0:64, kvh, slot * 128:slot * 128 + 128], ps[b][0:64, 0:ST], [("ps", b)], [("kTw", slot)])
            pieces.append(lambda: p_k(0))
            pieces.append(lambda: p_k(1))

            def p_c2():
                b = gbank()
                for kv in range(2):
                    for kvh in range(2):
                        o0 = (kv * 2 + kvh) * 32
                        for half in range(2):
                            combo = kv * 4 + kvh * 2 + half
                            mm(s, ps[b][0:64, o0:o0 + cnt], w2c[:, kv, half, :], hidT[:, combo * 32:combo * 32 + cnt],
                               half == 0, half == 1, ["w2c", "hidT"], [("ps", b)])
                for kvh in range(2):
                    vcopy(s, kTc[0:64, kvh, nlo:nhi], ps[b][0:64, kvh * 32:kvh * 32 + cnt], [("ps", b)], ["kTc"])
                    vcopy(s, vcT[:, kvh, nlo:nhi], ps[b][0:64, 64 + kvh * 32:64 + kvh * 32 + cnt], [("ps", b)], ["vcT"])
                for c in sorted(set([nlo // 128, (nhi - 1) // 128])):
                    b = gbank()
                    for kvh in range(2):
                        tr(s, psb[b][:, kvh * 64:(kvh + 1) * 64], vcT[:, kvh, c * 128:(c + 1) * 128], identb[0:64, 0:64],
                           ["vcT", "identb"], [("ps", b)])
                    vcopy(s, vc[:, c, :, 0:64], bcast(psb[b][:, 0:1], [[64, 2], [1, 64]]), [("ps", b)], ["vc"])
            pieces.append(p_c2)

            def p_tok3():
                b = gbank()
                for c in range(8):
                    mm(s, ps[b][:, 0:128], lhs_tok(c), winb[:, c, C_VS:C_VS + 128], c == 0, c == 7, ["win", "hT"], [("ps", b)])
                for c in range(8):
                    mm(s, ps[b][:, 128:280], lhs_tok(c), winb[:, c, C_VW:C_VW + 152], c == 0, c == 7, ["win", "hT"], [("ps", b)])
                vcopy(s, vs[:, i, :, 0:64], bcast(ps[b][:, 0:1], [[64, 2], [1, 64]]), [("ps", b)], [("vs", i)])
                vcopy(s, vw[:, i % 8, :, 0:64], bcast(ps[b][:, 128:129], [[64, 2], [1, 64]]), [("ps", b)], [("vw", i % 8)])
                act(s, gs[:, :], ps[b][:, 256:280], AF.Sigmoid, [("ps", b)], [("gsig", i % 2)])
            pieces.append(p_tok3)

            def p_tok_u():
                b = gbank()
                for c in range(8):
                    mm(s, ps[b][:, :], lhs_tok(c), winb[:, c, C_U:C_U + 512], c == 0, c == 7, ["win", "hT"], [("ps", b)])
                act(s, u_sb[:, :], ps[b][:, :], AF.Copy, [("ps", b)], ["u"])
            pieces.append(p_tok_u)

            def p_tok_v():
                b = gbank()
                for c in range(8):
                    mm(s, ps[b][:, :], lhs_tok(c), winb[:, c, C_V:C_V + 512], c == 0, c == 7, ["win", "hT"], [("ps", b)])
                act(s, v_f[:, :, :], ps[b][:, :], AF.Copy, [("ps", b)], ["vf"])
            pieces.append(p_tok_v)
            return pieces

        def p_uv():
            gelu_tanh(s, C, u_sb[:, :], u_sb[:, :], t512[:, :], "t512", ["u"], ["u"])
            gelu_tanh(s, C, v_f[:, :, :], v_f[:, :, :], t5, "t512", ["vf"], ["vf"])
            s.op(V, lambda e: e.tensor_reduce(st8[:, 0:8], v_f[:, :, :], AX.X, ALU.add), reads=["vf"], writes=["st8a"])
            vtt(s, t5, v_f[:, :, :], v_f[:, :, :], ALU.mult, ["vf"], ["t512"])
            s.op(V, lambda e: e.tensor_reduce(st8[:, 8:16], t5, AX.X, ALU.add), reads=["t512"], writes=["st8b"])
            vts(s, st8[:, 0:8], st8[:, 0:8], 1.0 / 64, None, ALU.mult, None, ["st8a"], ["st8a"])
            vtt(s, st8[:, 16:24], st8[:, 0:8], st8[:, 0:8], ALU.mult, ["st8a"], ["st8c"])
            vstt(s, st8[:, 8:16], st8[:, 8:16], 1.0 / 64, st8[:, 16:24], ALU.mult, ALU.subtract, ["st8b", "st8c"], ["st8b"])
            act(s, st8[:, 24:32], st8[:, 8:16], AF.Sqrt, ["st8b", "eps"], ["st8d"], bias=C.eps_t[:, 0:1])
            s.op(V, lambda e: e.reciprocal(st8[:, 32:40], st8[:, 24:32]), reads=["st8d"], writes=["st8e"])
            vtt(s, v_f[:, :, :], v_f[:, :, :], bcast(st8[:, 0:1], [[1, 8], [0, 64]]), ALU.subtract, ["vf", "st8a"], ["vf"])
            vtt(s, v_f[:, :, :], v_f[:, :, :], bcast(st8[:, 32:33], [[1, 8], [0, 64]]), ALU.mult, ["vf", "st8e"], ["vf"])
            vtt(s, v_f[:, :, :], v_f[:, :, :], bcast(vng[:, 0:1], [[64, 8], [1, 64]]), ALU.mult, ["vf", "vnp"], ["vf"])
            vtt(s, v_bf[:, :, :], v_f[:, :, :], bcast(vnb[:, 0:1], [[64, 8], [1, 64]]), ALU.add, ["vf", "vnp"], ["vbf"])

        def p_G():
            b = gbank()
            for g in range(8):
                mm(s, ps[b][:, g * 64:(g + 1) * 64], wsT[:, g, :], v_bf[:, g, :], True, True, ["wsT", "vbf"], [("ps", b)])
            for g in range(8):
                vstt(s, yraw[:, 8 + g, :], ps[b][:, g * 64:(g + 1) * 64], bsT[:, g:g + 1], u_sb[:, g * 64:(g + 1) * 64],
                     ALU.add, ALU.mult, [("ps", b), "bsT", "u"], ["yraw"])

        dma(s, "sync", xt[0][:, :, :], x_in[0:128, :].rearrange("(a p) d -> p a d", p=128), [], [("x", 0)])
        for p_ in make_P(0):
            p_()
        for i in range(NT):
            st = i
            a = 0
            T0 = i * 128
            xk = i % 3
            if i + 1 < NT:
                dma(s, "sync", xt[(i + 1) % 3][:, :, :], x_in[T0 + 128:T0 + 256, :].rearrange("(a p) d -> p a d", p=128),
                    [], [("x", (i + 1) % 3)])
            later.append(p_uv)
            later.append(p_G)
            if i + 1 < NT:
                later.extend(make_P(i + 1))

            need_topk = (2 * i + 2) > TOPK
            OCb, IMPb, OWb, OSb = (2, 4), (3, 5), (2, 4), (3, 5)

            def qr(kvh):
                return qT[i % 2][0:67, kvh, :, a * 128:(a + 1) * 128]

            for kvh in range(2):
                nmax = 8 * i + 6
                chunks = list(range(0, min(NCH - 1, nmax // 128) + 1))
                nmm = len(chunks) * 4
                for ci, c in enumerate(chunks):
                    def s_stage(c=c, kvh=kvh):
                        sbk = sbank()
                        partial = 128 * (c + 1) >= 8 * i
                        mm(s, s3(sbk), kTc[0:67, kvh, c * 128:(c + 1) * 128], qr(kvh), True, not partial,
                           ["kTc", "kTc_aug", ("qT", i % 2), ("qT_aug", i % 2)], [("ps", sbk)])
                        if partial:
                            dlt = 128 * i - 2048 * c
                            assert 0 <= dlt <= 2048
                            mm(s, s3(sbk), identb[:, :], bcast(mcmp[:, dlt:dlt + 1], [[0, 4], [1, 128]]), False, True,
                               ["identb", "mcmp"], [("ps", sbk)])
                        return exp_tile(sbk)

                    def pv_stage(k, c=c, kvh=kvh, ci=ci, nmm=nmm):
                        for g in range(4):
                            n_ = ci * 4 + g
                            mm(s, ps[OCb[kvh]][:, g * 65:(g + 1) * 65], pT[k][:, g * 128:(g + 1) * 128], vc[:, c, kvh, :],
                               n_ == 0, n_ == nmm - 1, [("pT", k), "vc"], [("ps", OCb[kvh])])
                            if need_topk:
                                mm(s, ps[IMPb[kvh]][:, g * 128:(g + 1) * 128], pT[k][:, g * 128:(g + 1) * 128], ovl[:, c, :],
                                   n_ == 0, n_ == nmm - 1, [("pT", k), "ovl"], [("ps", IMPb[kvh])])
                    pair(s_stage, pv_stage)
                flush()
                vcopy(s, oc_sb[:, kvh, :], ps[OCb[kvh]][:, 0:260], [("ps", OCb[kvh])], [("oc", kvh)])
                if need_topk:
                    W = 2 * i + 2
                    PI = IMPb[kvh]
                    vts(s, rc[:, 0:4], bcast(oc_sb[:, kvh, 64:65], [[65, 4]]), 1e-30, None, ALU.add, None, [("oc", kvh)], ["rc"])
                    s.op(V, lambda e: e.reciprocal(rc[:, 4:8], rc[:, 0:4]), reads=["rc"], writes=["rci"])
                    vts(s, impsb[:, 0:W], ps[PI][:, 0:W], rc[:, 4:5], None, ALU.mult, None, [("ps", PI), "rci"], ["imp"])
                    for g in range(1, 4):
                        vstt(s, impsb[:, 0:W], ps[PI][:, g * 128:g * 128 + W], rc[:, 4 + g:5 + g], impsb[:, 0:W],
                             ALU.mult, ALU.add, [("ps", PI), "rci", "imp"], ["imp"])
                    s.op(V, lambda e: e.memset(impsb[:, 0:1], 1e6), reads=["imp"], writes=["imp"])
                    s.op(V, lambda e, i=i: e.memset(impsb[0:64, 2 * i - 1:2 * i + 1], 1e6), reads=["imp"], writes=["imp"])
                    s.op(V, lambda e, i=i: e.memset(impsb[0:64, 2 * i + 1:2 * i + 2], -1e30), reads=["imp"], writes=["imp"])
                    s.op(V, lambda e, i=i: e.memset(impsb[64:128, 2 * i:2 * i + 2], 1e6), reads=["imp"], writes=["imp"])
                    s.op(V, lambda e, W=W: e.max(m8[:, 0:8], impsb[:, 0:W]), reads=["imp"], writes=["m8a"])
                    s.op(V, lambda e, W=W: e.match_replace(imptmp[:, 0:W], m8[:, 0:8], impsb[:, 0:W], -1e30),
                         reads=["imp", "m8a"], writes=["imptmp"])
                    s.op(V, lambda e, W=W: e.max(m8[:, 8:16], imptmp[:, 0:W]), reads=["imptmp"], writes=["m8b"])
                    vts(s, nmq[:, 0:W], impsb[:, 0:W], m8[:, 15:16], NEGM, ALU.is_lt, ALU.mult, ["imp", "m8b"], ["nmq"])
                    b = gbank()
                    tr(s, psb[b][:, 0:128], nmq[:, :], identb[:, :], ["nmq", "identb"], [("ps", b)])
                    vcopy(s, nmT[:, kvh, :], psb[b][:, 0:128], [("ps", b)], [("nmT", kvh)])
            for kvh in range(2):
                kts = list(range(max(0, i - 4), i + 1))
                nmm = len(kts) * 4
                for ci, kt in enumerate(kts):
                    def s_stage(kt=kt, kvh=kvh):
                        sbk = sbank()
                        rs = (kt % 8) * 128
                        extra = (kt == i) or (kt == i - 4)
                        mm(s, s3(sbk), kTw[0:67, kvh, rs:rs + 128], qr(kvh), True, not extra,
                           [("kTw", kt % 8), ("kTw_aug", kt % 8), ("qT", i % 2), ("qT_aug", i % 2)], [("ps", sbk)])
                        if kt == i:
                            mm(s, s3(sbk), identb[:, :], bcast(mdiag[:, 0:1], [[0, 4], [1, 128]]), False, True,
                               ["identb", "mdiag"], [("ps", sbk)])
                        elif kt == i - 4:
                            mm(s, s3(sbk), identb[:, :], bcast(mtail[:, 0:1], [[0, 4], [1, 128]]), False, True,
                               ["identb", "mtail"], [("ps", sbk)])
                        return exp_tile(sbk)

                    def pv_stage(k, kt=kt, kvh=kvh, ci=ci, nmm=nmm):
                        for g in range(4):
                            n_ = ci * 4 + g
                            mm(s, ps[OWb[kvh]][:, g * 65:(g + 1) * 65], pT[k][:, g * 128:(g + 1) * 128], vw[:, kt % 8, kvh, :],
                               n_ == 0, n_ == nmm - 1, [("pT", k), ("vw", kt % 8), "vw1"], [("ps", OWb[kvh])])
                    pair(s_stage, pv_stage)
            SEL_NEAR = 16
            sel_kts = [list(range(i + 1)) if kvh == 1 or i < SEL_NEAR + 1 else [0] + list(range(i - SEL_NEAR + 1, i + 1))
                       for kvh in range(2)]
            tot_sel = len(sel_kts[0]) + len(sel_kts[1])
            done_sel = 0
            for kvh in range(2):
                nmm = len(sel_kts[kvh]) * 4
                for ci, kt in enumerate(sel_kts[kvh]):
                    def s_stage(kt=kt, kvh=kvh):
                        sbk = sbank()
                        last_is_qk = (not need_topk) and kt != i
                        mm(s, s3(sbk), kTs[0:67, kvh, kt * 128:(kt + 1) * 128], qr(kvh), True, last_is_qk,
                           [("kTs", kt // NSUB), "kTs_aug", ("qT", i % 2), ("qT_aug", i % 2)], [("ps", sbk)])
                        if need_topk:
                            m_ = kt // 32
                            e0 = 128 * (kt - 32 * m_)
                            mm(s, s3(sbk), esel[64 * m_:64 * m_ + 64, e0:e0 + 128],
                               bcast(nmT[64 * m_:64 * m_ + 64, kvh, 0:1], [[0, 4], [1, 128]]), False, kt != i,
                               ["esel", ("nmT", kvh)], [("ps", sbk)])
                        if kt == i:
                            mm(s, s3(sbk), identb[:, :], bcast(mdiag[:, 0:1], [[0, 4], [1, 128]]), False, True,
                               ["identb", "mdiag"], [("ps", sbk)])
                        return exp_tile(sbk)

                    def pv_stage(k, kt=kt, kvh=kvh, nmm=nmm, ci=ci):
                        for g in range(4):
                            n_ = ci * 4 + g
                            mm(s, ps[OSb[kvh]][:, g * 65:(g + 1) * 65], pT[k][:, g * 128:(g + 1) * 128], vs[:, kt, kvh, :],
                               n_ == 0, n_ == nmm - 1, [("pT", k), ("vs", kt), "vs1"], [("ps", OSb[kvh])])
                    rem = tot_sel - done_sel
                    done_sel += 1
                    pair(s_stage, pv_stage, npop=-(-len(later) // max(1, rem - 1)))
            flush()
            flush_later()
            for kvh in range(2):
                d0 = kvh * 12
                srcs = (oc_sb[:, kvh, :], ps[OWb[kvh]][:, 0:260], ps[OSb[kvh]][:, 0:260])
                keys = (("oc", kvh), ("ps", OWb[kvh]), ("ps", OSb[kvh]))
                order = (0, 2, 1)
                for br in range(3):
                    sidx = order[br]
                    vts(s, bcast(den[:, d0 + br:d0 + br + 1], [[3, 4]]), bcast(srcs[sidx][:, 64:65], [[65, 4]]), 1e-30, None,
                        ALU.add, None, [keys[sidx]], [("den", kvh)])
                s.op(V, lambda e, d0=d0: e.reciprocal(den[:, d0:d0 + 12], den[:, d0:d0 + 12]), reads=[("den", kvh)], writes=[("den", kvh)])
                vtt(s, coef[:, d0:d0 + 12], den[:, d0:d0 + 12], gsig[i % 2][:, d0:d0 + 12], ALU.mult, [("den", kvh), ("gsig", i % 2)], [("coef", kvh)])
                yv = yraw[:, kvh * 4:(kvh + 1) * 4, :]
                cb = lambda br: bcast(coef[:, d0 + br:d0 + br + 1], [[3, 4], [0, 64]])
                v4 = lambda ap: bcast(ap[:, 0:1], [[65, 4], [1, 64]])
                vtt(s, yv, v4(srcs[0]), cb(0), ALU.mult, [keys[0], ("coef", kvh)], ["yraw"])
                vtt(s, ctmp[:, :, :], v4(srcs[2]), cb(1), ALU.mult, [keys[2], ("coef", kvh)], ["ctmp"])
                vtt(s, yv, yv, ctmp[:, :, :], ALU.add, ["yraw", "ctmp"], ["yraw"])
                vtt(s, ctmp[:, :, :], v4(srcs[1]), cb(2), ALU.mult, [keys[1], ("coef", kvh)], ["ctmp"])
                vtt(s, yv, yv, ctmp[:, :, :], ALU.add, ["yraw", "ctmp"], ["yraw"])


            later.extend(make_R(i, xk))
        flush_later()
        s.run()


_CACHE = {}


def kernel(**inputs):
    x = np.asarray(inputs["x"], dtype=np.float32)
    B, S, _ = x.shape
    shapes = {n: tuple(np.asarray(inputs[n]).shape) for n in W_NAMES}
    key = (B, S)
    if key not in _CACHE:
        _CACHE[key] = build(S, DEPTH, shapes, mode="full")
    nc = _CACHE[key]
    consts = make_consts(S)
    in_maps = []
    for b in range(B):
        m = {"x": np.ascontiguousarray(x[b]), "c": np.ascontiguousarray(np.asarray(inputs["c"], dtype=np.float32)[b])}
        for n in W_NAMES:
            m[n] = np.ascontiguousarray(np.asarray(inputs[n], dtype=np.float32))
        m.update(consts)
        in_maps.append(m)
    res = run_bass_kernel_spmd(nc, in_maps, core_ids=list(range(B)))
    return np.stack([np.asarray(r["out"], dtype=np.float32) for r in res.results], axis=0)
```

```python
import contextlib
import numpy as np
import concourse.bass as bass
import concourse.mybir as mybir
from concourse.bass_utils import run_bass_kernel_spmd

F32 = mybir.dt.float32
BF16 = mybir.dt.bfloat16
AF = mybir.ActivationFunctionType
ALU = mybir.AluOpType
AX = mybir.AxisListType

D = 1024
DEPTH = 2
SEQ = 8192
NB = 8
DFF = 2816
INW = 2328
ALPHA = (2.0 * DEPTH) ** 0.25
EPS = 1e-5
NEGM = -32768.0
C_Q, C_KC, C_VC, C_KS, C_VS, C_KW, C_VW, C_GT, C_U, C_V = 0, 512, 640, 768, 896, 1024, 1152, 1280, 1304, 1816
GC0 = 1.5957691216057308
GC1 = GC0 * 0.044715


class Sched:
    ENGS = ("tensor", "vector", "scalar", "gpsimd", "sync")
    EPOCH = 30000
    KD = 8

    def __init__(self, nc, stack, tag):
        self.nc = nc
        self.ops = {e: [] for e in self.ENGS}
        self.lastw = {}
        self.readers = {}
        self.tag = tag
        self.stack = stack
        self.dmas = {e: [] for e in self.ENGS}
        self.dma_sems = {}
        self.eng_sems = {}
        self.marks = []
        self.all_ops = []

    def _dep(self, o, d):
        if d is None or d is o:
            return
        if (not d["dma"]) and d["eng"] == "tensor" and o["eng"] == "tensor" and not o["dma"]:
            return
        o["deps"].append(d)

    def op(self, eng, fn, reads=(), writes=(), dma=False):
        o = dict(eng=eng, fn=fn, deps=[], dma=dma, flag=False, n=0, idx=len(self.ops[eng]), seq=len(self.all_ops), clock={})
        for r in reads:
            self._dep(o, self.lastw.get(r))
        for w in writes:
            self._dep(o, self.lastw.get(w))
            rd = self.readers.get(w)
            if rd:
                for d in rd["eng"].values():
                    self._dep(o, d)
                for d in rd["dma"]:
                    self._dep(o, d)
        for r in reads:
            rd = self.readers.setdefault(r, dict(eng={}, dma=[]))
            if dma:
                rd["dma"].append(o)
            else:
                rd["eng"][eng] = o
        for w in writes:
            self.lastw[w] = o
            self.readers[w] = dict(eng={}, dma=[])
        if dma:
            lst = self.dmas[eng]
            o["dq"] = len(lst)
            if len(lst) >= self.KD:
                o["deps"].append(lst[len(lst) - self.KD])
            lst.append(o)
        self.ops[eng].append(o)
        self.all_ops.append(o)
        return o

    def _key(self, d):
        if d["dma"]:
            return ("d", d["eng"], d["dq"] % self.KD), 16 * (d["dq"] // self.KD + 1)
        return ("e", d["eng"]), d["n"]

    def finalize(self):
        for o in self.all_ops:
            for d in o["deps"]:
                if not d["dma"]:
                    d["flag"] = True
        for e in self.ENGS:
            n = 0
            for o in self.ops[e]:
                if o["flag"] and not o["dma"]:
                    n += 1
                    o["n"] = n
            nep = max(1, -(-n // self.EPOCH))
            self.eng_sems[e] = [self.stack.enter_context(self.nc.semaphore(f"{self.tag}_{e}_{k}")) for k in range(nep)]
            if self.dmas[e]:
                self.dma_sems[e] = [self.stack.enter_context(self.nc.semaphore(f"{self.tag}_d{e}_{k}")) for k in range(self.KD)]
        know = {e: {} for e in self.ENGS}
        for o in self.all_ops:
            kn = know[o["eng"]]
            need = []
            for d in sorted(o["deps"], key=lambda d: -d["seq"]):
                key, val = self._key(d)
                if kn.get(key, 0) >= val:
                    continue
                need.append(d)
                for k2, v2 in d["clock"].items():
                    if kn.get(k2, 0) < v2:
                        kn[k2] = v2
                kn[key] = val
            o["waits"] = need
            if o["dma"] or o["flag"]:
                ck = dict(kn)
                if not o["dma"]:
                    ck[("e", o["eng"])] = max(ck.get(("e", o["eng"]), 0), o["n"])
                o["clock"] = ck

    def emit(self, ename, eng):
        def wait_for(d):
            if d["dma"]:
                val = 16 * (d["dq"] // self.KD + 1)
                eng.wait_ge(self.dma_sems[d["eng"]][d["dq"] % self.KD], val)
            else:
                ep = (d["n"] - 1) // self.EPOCH
                eng.wait_ge(self.eng_sems[d["eng"]][ep], (d["n"] - 1) % self.EPOCH + 1)

        for o in self.ops[ename]:
            for d in o["waits"]:
                wait_for(d)
            ins = o["fn"](eng)
            if o["dma"]:
                ins.then_inc(self.dma_sems[ename][o["dq"] % self.KD], 16)
            elif o["flag"]:
                ep = (o["n"] - 1) // self.EPOCH
                ins.then_inc(self.eng_sems[ename][ep], 1)
        for e in self.ENGS:
            for o in self.dmas[e][-self.KD:]:
                wait_for(o)

    def run(self):
        self.finalize()
        nc = self.nc
        with nc.Block() as block:
            @block.tensor
            def _(e):
                self.emit("tensor", e)

            @block.vector
            def _(e):
                self.emit("vector", e)

            @block.scalar
            def _(e):
                self.emit("scalar", e)

            @block.gpsimd
            def _(e):
                self.emit("gpsimd", e)

            @block.sync
            def _(e):
                self.emit("sync", e)


class RowBuf:
    def __init__(self, aps, rows):
        self.aps = aps
        self.rows = rows

    def __getitem__(self, key):
        rs, cs = key
        z = rs.start // self.rows
        assert (rs.stop - 1) // self.rows == z
        return self.aps[z][rs.start - z * self.rows:rs.stop - z * self.rows, cs]


def bcast(ap, pat):
    return bass.AP(ap.tensor, ap.offset, [list(ap.ap[0])] + [list(p) for p in pat])


class Ctx:
    pass


def sb(nc, stack, name, shape, dt):
    return stack.enter_context(nc.sbuf_tensor(name, list(shape), dt))


def mm(s, out, lhsT, rhs, start, stop, reads, writes):
    s.op("tensor", lambda e: e.matmul(out, lhsT, rhs, start=start, stop=stop), reads=reads, writes=writes)


def tr(s, out, in_, ident, reads, writes):
    s.op("tensor", lambda e: e.transpose(out, in_, ident), reads=reads, writes=writes)


def act(s, out, in_, func, reads, writes, bias=None, scale=None):
    kw = {}
    if bias is not None:
        kw["bias"] = bias
    if scale is not None:
        kw["scale"] = scale
    s.op("scalar", lambda e: e.activation(out, in_, func, **kw), reads=reads, writes=writes)


def vtt(s, out, in0, in1, op, reads, writes, eng="vector"):
    s.op(eng, lambda e: e.tensor_tensor(out, in0, in1, op), reads=reads, writes=writes)


def vts(s, out, in0, s1, s2, op0, op1, reads, writes, eng="vector"):
    if op1 is None:
        s.op(eng, lambda e: e.tensor_scalar(out, in0, s1, None, op0), reads=reads, writes=writes)
    else:
        s.op(eng, lambda e: e.tensor_scalar(out, in0, s1, s2, op0, op1), reads=reads, writes=writes)


def vstt(s, out, in0, sc, in1, op0, op1, reads, writes):
    s.op("vector", lambda e: e.scalar_tensor_tensor(out, in0, sc, in1, op0, op1), reads=reads, writes=writes)


def vcopy(s, out, in_, reads, writes, eng="vector"):
    s.op(eng, lambda e: e.tensor_copy(out, in_), reads=reads, writes=writes)


def dma(s, q, out, in_, reads, writes, **kw):
    s.op(q, lambda e: e.dma_start(out, in_, **kw), reads=reads, writes=writes, dma=True)


def gelu_tanh(s, C, out, x, tmp, shape_key, reads, writes):
    tk = shape_key
    vtt(s, tmp, x, x, ALU.mult, reads=reads, writes=[tk])
    vts(s, tmp, tmp, GC1, GC0, ALU.mult, ALU.add, reads=[tk], writes=[tk])
    vtt(s, tmp, tmp, x, ALU.mult, reads=[tk] + list(reads), writes=[tk])
    act(s, tmp, tmp, AF.Sigmoid, reads=[tk], writes=[tk])
    vtt(s, out, tmp, x, ALU.mult, reads=[tk] + list(reads), writes=writes)


def layer_norm_tile(s, C, r, out, gb, bb, key_r, key_out, tagk):
    st = C.ln_stats
    kst = ("lnst",)
    for h in range(2):
        s.op("vector", lambda e, h=h: e.bn_stats(st[:, h * 6:(h + 1) * 6], r[:, h * 512:(h + 1) * 512]),
             reads=[key_r], writes=[kst] if h == 0 else [kst])
    s.op("vector", lambda e: e.bn_aggr(C.ln_mv[:, 0:2], st[:, 0:12]), reads=[kst], writes=[("lnmv",)])
    act(s, C.ln_mv[:, 2:3], C.ln_mv[:, 1:2], AF.Sqrt, reads=[("lnmv",)], writes=[("lnrs",)], bias=C.eps_t[:, 0:1])
    s.op("vector", lambda e: e.reciprocal(C.ln_mv[:, 3:4], C.ln_mv[:, 2:3]), reads=[("lnrs",)], writes=[("lnri",)])
    vts(s, r, r, C.ln_mv[:, 0:1], C.ln_mv[:, 3:4], ALU.subtract, ALU.mult, reads=[key_r, ("lnmv",), ("lnri",)], writes=[key_r])
    vtt(s, r, r, gb, ALU.mult, reads=[key_r, tagk], writes=[key_r])
    vtt(s, out, r, bb, ALU.add, reads=[key_r, tagk], writes=[key_out])


def phase_ada(nc, G, S, L):
    with contextlib.ExitStack() as stack:
        s = Sched(nc, stack, "a")
        cT = sb(nc, stack, "a_cT", [128, 8], F32)
        cTs = sb(nc, stack, "a_cTs", [128, 8], F32)
        wa = [sb(nc, stack, f"a_wa{k}", [128, 8, 512], F32) for k in range(2)]
        brow = [sb(nc, stack, f"a_br{k}", [1, 512], F32) for k in range(2)]
        row = [sb(nc, stack, f"a_row{k}", [1, 512], F32) for k in range(2)]
        ones = sb(nc, stack, "a_ones", [1, 128], F32)
        gtile = [sb(nc, stack, f"a_gt{k}", [128, 512], F32) for k in range(2)]
        ps = [stack.enter_context(nc.psum_tensor(f"a_ps{k}", [128, 512], F32)) for k in range(4)]
        pT = stack.enter_context(nc.psum_tensor("a_pT", [128, 512], F32))
        dma(s, "sync", cT[:, :], G.c_d.rearrange("(c p) -> p c", p=128), reads=[], writes=["cT"],
            allow_slow_non_contiguous=True)
        act(s, cTs[:, :], cT[:, :], AF.Silu, reads=["cT"], writes=["cTs"])
        s.op("vector", lambda e: e.memset(ones[:, :], 1.0), writes=["ones"])
        it = 0
        for l in range(L):
            for j in range(12):
                k = it % 2
                it += 1
                dma(s, "sync", wa[k][:, :, :],
                    G.w_ada[l].rearrange("(c p) n -> p c n", p=128)[:, :, j * 512:(j + 1) * 512],
                    reads=[], writes=[("wa", k)])
                dma(s, "sync", brow[k][:, :], G.b_ada[l:l + 1, j * 512:(j + 1) * 512], reads=[], writes=[("br", k)])
                for c in range(8):
                    mm(s, ps[k][0:1, :], cTs[:, c:c + 1], wa[k][:, c, :], c == 0, c == 7,
                       reads=["cTs", ("wa", k)], writes=[("ps", k)])
                vtt(s, row[k][:, :], ps[k][0:1, :], brow[k][:, :], ALU.add, reads=[("ps", k), ("br", k)], writes=[("row", k)])
                for q in range(4):
                    col = l * 48 + j * 4 + q
                    mm(s, pT[:, col * 2:col * 2 + 2], row[k][0:1, q * 128:(q + 1) * 128], ones[0:1, 0:2], True, True,
                       reads=[("row", k), "ones"], writes=["pT"])
                if j in (4, 5, 10, 11):
                    mm(s, ps[2 + k][:, :], ones[0:1, 0:128], row[k][0:1, :], True, True,
                       reads=[("row", k), "ones"], writes=[("ps", 2 + k)])
                    vts(s, gtile[k][:, :], ps[2 + k][:, :], 1.0, None, ALU.add, None, reads=[("ps", 2 + k)], writes=[("gt", k)])
                    which = 0 if j < 6 else 1
                    half = j % 2
                    dma(s, "sync", G.gb_d[l, which, :, half * 512:(half + 1) * 512], gtile[k][:, :],
                        reads=[("gt", k)], writes=["gb_d"])
        vcopy(s, G.modT[:, 0:L * 48], bcast(pT[:, 0:1], [[2, L * 48]]), reads=["pT"], writes=["modT"])
        for l in range(L):
            for j in (1, 4):
                a0 = l * 48 + j * 8
                vts(s, G.modT[:, a0:a0 + 8], G.modT[:, a0:a0 + 8], 1.0, None, ALU.add, None, reads=["modT"], writes=["modT"])
        s.run()


def phase_ffn(nc, G, S, l, x_in, x_out):
    ST = 256
    NSUB = ST // 128
    NST = S // ST
    with contextlib.ExitStack() as stack:
        s = Sched(nc, stack, f"f{l}")
        C = Ctx()
        w1b = sb(nc, stack, f"f{l}_w1", [128, 8, DFF], BF16)
        w3b = sb(nc, stack, f"f{l}_w3", [128, 8, DFF], BF16)
        w2b = sb(nc, stack, f"f{l}_w2", [128, 22, D], BF16)
        lng = sb(nc, stack, f"f{l}_lng", [128, D], F32)
        lnb = sb(nc, stack, f"f{l}_lnb", [128, D], F32)
        g2b = sb(nc, stack, f"f{l}_g2b", [128, D], F32)
        xt = [sb(nc, stack, f"f{l}_x{k}", [128, NSUB, D], F32) for k in range(2)]
        hTs = [sb(nc, stack, f"f{l}_hT{k}", [128, 8, ST], BF16) for k in range(2)]
        gT = sb(nc, stack, f"f{l}_gT", [128, 22, ST], BF16)
        sa = [sb(nc, stack, f"f{l}_sa{k}", [128, ST], F32) for k in range(2)]
        rt = [sb(nc, stack, f"f{l}_r{k}", [128, D], F32) for k in range(2)]
        ot = [sb(nc, stack, f"f{l}_o{k}", [128, D], F32) for k in range(2)]
        ident = sb(nc, stack, f"f{l}_id", [128, 128], F32)
        C.ln_stats = sb(nc, stack, f"f{l}_lst", [128, 12], F32)
        C.ln_mv = sb(nc, stack, f"f{l}_lmv", [128, 4], F32)
        C.eps_t = sb(nc, stack, f"f{l}_eps", [128, 1], F32)
        ps = [stack.enter_context(nc.psum_tensor(f"f{l}_ps{k}", [128, 512], F32)) for k in range(8)]
        scT = G.modT[:, l * 48 + 32:l * 48 + 40]
        shT = G.modT[:, l * 48 + 24:l * 48 + 32]

        s.op("vector", lambda e: e.memset(C.eps_t[:, :], EPS), writes=["eps"])
        dma(s, "sync", ident[:, :], G.c_ident[:, :], reads=[], writes=["ident"])
        dma(s, "sync", lng[:, :], G.ln2_g[l:l + 1, :].partition_broadcast(128), reads=[], writes=["lnp"])
        dma(s, "sync", lnb[:, :], G.ln2_b[l:l + 1, :].partition_broadcast(128), reads=[], writes=["lnp"])
        dma(s, "sync", g2b[:, :], G.gb_d[l, 1, :, :], reads=[], writes=["g2b"])
        for c in range(8):
            for h in range(2):
                dma(s, "gpsimd", w1b[:, c, h * 1408:(h + 1) * 1408], G.w1[l, c * 128:(c + 1) * 128, h * 1408:(h + 1) * 1408],
                    reads=[], writes=["w1"])
                dma(s, "gpsimd", w3b[:, c, h * 1408:(h + 1) * 1408], G.w3[l, c * 128:(c + 1) * 128, h * 1408:(h + 1) * 1408],
                    reads=[], writes=["w3"])
        for f in range(22):
            dma(s, "gpsimd", w2b[:, f, :], G.w2[l, f * 128:(f + 1) * 128, :], reads=[], writes=["w2"])

        pi = [0]

        def nps():
            pi[0] = (pi[0] + 1) % 8
            return pi[0]

        def emit_tr(st_):
            k_ = st_ % 2
            for a in range(NSUB):
                for c4 in range(2):
                    b = nps()
                    for cc in range(4):
                        c = c4 * 4 + cc
                        tr(s, ps[b][:, cc * 128:(cc + 1) * 128], xt[k_][:, a, c * 128:(c + 1) * 128], ident[:, :],
                           reads=[("x", k_), "ident"], writes=[("ps", b)])
                    for cc in range(4):
                        c = c4 * 4 + cc
                        act(s, hTs[k_][:, c, a * 128:(a + 1) * 128], ps[b][:, cc * 128:(cc + 1) * 128], AF.Identity,
                            reads=[("ps", b), "modT"], writes=[("hT", k_)], bias=shT[:, c:c + 1], scale=scT[:, c:c + 1])

        dma(s, "sync", xt[0][:, :, :], x_in[0:ST, :].rearrange("(a p) d -> p a d", p=128), reads=[], writes=[("x", 0)])
        emit_tr(0)
        for st in range(NST):
            k = st % 2
            T0 = st * ST
            if st + 1 < NST:
                dma(s, "sync", xt[1 - k][:, :, :], x_in[T0 + ST:T0 + 2 * ST, :].rearrange("(a p) d -> p a d", p=128),
                    reads=[], writes=[("x", 1 - k)])
            hT = hTs[k]
            hk = ("hT", k)
            for f in range(22):
                ba = nps()
                for c in range(8):
                    mm(s, ps[ba][:, 0:ST], w1b[:, c, f * 128:(f + 1) * 128], hT[:, c, :], c == 0, c == 7,
                       reads=["w1", hk], writes=[("ps", ba)])
                bb = nps()
                for c in range(8):
                    mm(s, ps[bb][:, 0:ST], w3b[:, c, f * 128:(f + 1) * 128], hT[:, c, :], c == 0, c == 7,
                       reads=["w3", hk], writes=[("ps", bb)])
                kk = f % 2
                act(s, sa[kk][:, :], ps[ba][:, 0:ST], AF.Silu, reads=[("ps", ba)], writes=[("sa", kk)])
                vtt(s, gT[:, f, :], sa[kk][:, :], ps[bb][:, 0:ST], ALU.mult, reads=[("sa", kk), ("ps", bb)], writes=["gT"])
            if st + 1 < NST:
                emit_tr(st + 1)
            for a in range(NSUB):
                kr = (st * NSUB + a) % 2
                for h in range(2):
                    b = nps()
                    for f in range(22):
                        mm(s, ps[b][:, :], gT[:, f, a * 128:(a + 1) * 128], w2b[:, f, h * 512:(h + 1) * 512], f == 0, f == 21,
                           reads=["gT", "w2"], writes=[("ps", b)])
                    vtt(s, rt[kr][:, h * 512:(h + 1) * 512], ps[b][:, :], g2b[:, h * 512:(h + 1) * 512], ALU.mult,
                        reads=[("ps", b), "g2b"], writes=[("r", kr)])
                vstt(s, rt[kr][:, :], xt[k][:, a, :], ALPHA, rt[kr][:, :], ALU.mult, ALU.add,
                     reads=[("x", k), ("r", kr)], writes=[("r", kr)])
                layer_norm_tile(s, C, rt[kr][:, :], ot[kr][:, :], lng[:, :], lnb[:, :], ("r", kr), ("o", kr), "lnp")
                t0 = T0 + a * 128
                dma(s, "sync", x_out[t0:t0 + 128, :], ot[kr][:, :], reads=[("o", kr)], writes=["xout"])
        s.run()


W_NAMES = ["w_ada", "b_ada", "w_in", "cmp_pos", "cmp_w1", "cmp_w2", "vn_g", "vn_b", "w_s", "b_s",
           "out_g", "w_o", "ln1_g", "ln1_b", "w1", "w3", "w2", "ln2_g", "ln2_b"]


def build(S, L, shapes, mode="full"):
    nc = bass.Bass("TRN2", target_bir_lowering=False)
    G = Ctx()
    G.x_d = nc.dram_tensor("x", [S, D], F32, kind="ExternalInput").ap()
    G.c_d = nc.dram_tensor("c", [D], F32, kind="ExternalInput").ap()
    for n in W_NAMES:
        setattr(G, n, nc.dram_tensor(n, list(shapes[n]), F32, kind="ExternalInput").ap())
    for n, shp in const_shapes(S).items():
        setattr(G, n, nc.dram_tensor(n, list(shp), F32, kind="ExternalInput").ap())
    G.out_d = nc.dram_tensor("out", [S, D], F32, kind="ExternalOutput").ap()
    G.gb_d = nc.dram_tensor("gb_d", [L, 2, 128, D], F32, kind="Internal").ap()
    G.xs = [RowBuf([nc.dram_tensor(f"xs{k}_{z}", [min(S, 2048), D], F32, kind="Internal").ap()
                    for z in range(max(1, S // 2048))], 2048) for k in range(2)]
    with contextlib.ExitStack() as top:
        G.modT = sb(nc, top, "modT", [128, L * 48], F32)
        phase_ada(nc, G, S, L)
        nc.all_engine_barrier()
        if mode == "ffn":
            phase_ffn(nc, G, S, 0, RowBuf([G.x_d], S), RowBuf([G.out_d], S))
        elif mode == "mix":
            phase_mix(nc, G, S, 0, RowBuf([G.x_d], S), RowBuf([G.out_d], S))
        else:
            cur = RowBuf([G.x_d], S)
            for l in range(L):
                phase_mix(nc, G, S, l, cur, G.xs[0])
                nc.all_engine_barrier()
                dst = RowBuf([G.out_d], S) if l == L - 1 else G.xs[1]
                phase_ffn(nc, G, S, l, G.xs[0], dst)
                nc.all_engine_barrier()
                cur = dst
    return nc


def const_shapes(S):
    NCP = S // 16
    return {"c_ident": (128, 128), "c_kaug": (3, S), "c_caug": (3, NCP), "c_qaug": (3, 2, 4, S),
            "c_ovl": (128, max(1, NCP // 128), 128), "c_mcmp": (128, 2176), "c_mdiag": (128, 128),
            "c_mtail": (128, 128), "c_tril": (128, 128), "c_esel": (128, 4096)}


def make_consts(S):
    NCP = S // 16
    NCH = max(1, NCP // 128)
    t = np.arange(S)
    kaug = np.stack([t // 128, t % 128, np.ones(S)]).astype(np.float32)
    pos = 16 * np.arange(NCP) + 31
    caug = np.stack([pos // 128, pos % 128, np.ones(NCP)]).astype(np.float32)
    qaug = np.zeros((3, 2, 4, S), np.float32)
    for kvh in range(2):
        for g in range(4):
            sl = 2.0 ** (-(kvh * 4 + g + 1))
            qaug[0, kvh, g] = 128 * sl
            qaug[1, kvh, g] = sl
            qaug[2, kvh, g] = -sl * (128 * (t // 128) + 64)
    n = np.arange(NCH * 128)
    j = np.arange(128)
    ov = ((16 * n[:, None] <= 64 * j[None, :] + 63) & (16 * n[:, None] + 31 >= 64 * j[None, :]) & (n[:, None] < NCP - 1))
    ovl = ov.astype(np.float32).reshape(NCH, 128, 128).transpose(1, 0, 2).copy()
    p = np.arange(128)
    u = np.arange(2176)
    mcmp = np.where(u[None, :] >= 16 * p[:, None] + 31, 0.0, NEGM).astype(np.float32)
    mdiag = np.where(p[:, None] <= p[None, :], 0.0, NEGM).astype(np.float32)
    mtail = np.where(p[:, None] > p[None, :], 0.0, NEGM).astype(np.float32)
    tril = (p[:, None] <= p[None, :]).astype(np.float32)
    w = np.arange(4096)
    esel = ((p[:, None] % 64) == (w[None, :] // 64)).astype(np.float32)
    return {"c_esel": esel, "c_ident": np.eye(128, dtype=np.float32), "c_kaug": kaug, "c_caug": caug, "c_qaug": qaug,
            "c_ovl": ovl, "c_mcmp": mcmp, "c_mdiag": mdiag, "c_mtail": mtail, "c_tril": tril}


def phase_mix(nc, G, S, l, x_in, x_out):
    ST = 128
    NSUB = ST // 128
    NB_ = ST // 16
    NST = S // ST
    NT = S // 128
    NCP = S // 16
    NCH = max(1, NCP // 128)
    NSLC = S // 64
    TOPK = min(16, NSLC)
    with contextlib.ExitStack() as stack:
        s = Sched(nc, stack, f"m{l}")
        C = Ctx()
        T = lambda name, shape, dt: sb(nc, stack, f"m{l}_" + name, shape, dt)
        winb = T("win", [128, 8, INW], BF16)
        wob = T("wo", [128, 8, D], BF16)
        w1c = T("w1c", [128, 2, 16, 256], BF16)
        w2c = T("w2c", [128, 2, 2, 64], BF16)
        posTf = T("posTf", [128, 2, 16], F32)
        posT = T("posT", [128, 2, 16], BF16)
        posb = T("posb", [128, 4], F32)
        wsT = T("wsT", [128, 8, 128], BF16)
        bsT = T("bsT", [128, 8], F32)
        kTs = T("kTs", [67, 2, S], BF16)
        vs = T("vs", [128, NT, 2, 65], BF16)
        kTc = T("kTc", [67, 2, NCP], BF16)
        vcT = T("vcT", [64, 2, NCP], BF16)
        vc = T("vc", [128, NCH, 2, 65], BF16)
        kTw = T("kTw", [67, 2, 1024], BF16)
        vw = T("vw", [128, 8, 2, 65], BF16)
        identf = T("idf", [128, 128], F32)
        identb = T("idb", [128, 128], BF16)
        mcmp = T("mcmp", [128, 2176], BF16)
        mdiag = T("mdiag", [128, 128], BF16)
        mtail = T("mtail", [128, 128], BF16)
        tril = T("tril", [128, 128], F32)
        esel = T("esel", [128, 4096], BF16)
        ovl = T("ovl", [128, NCH, 128], BF16)
        ln1g = T("ln1g", [128, D], F32)
        ln1b = T("ln1b", [128, D], F32)
        vng = T("vng", [128, 512], F32)
        vnb = T("vnb", [128, 512], F32)
        outgT = T("outgT", [128, 8], F32)
        xt = [T(f"xt{k}", [128, NSUB, D], F32) for k in range(3)]
        hT = T("hT", [128, 8, ST], BF16)
        qT = [T(f"qT{k}", [67, 2, 4, ST], BF16) for k in range(2)]
        kcb = T("kcb", [128, 2, 2, ST + 16], BF16)
        hidT = T("hidT", [128, 256], BF16)
        pT = [T(f"pT{k}", [128, 512], BF16) for k in range(3)]
        nmq = T("nmq", [128, 128], BF16)
        nmT = T("nmT", [128, 2, 128], BF16)
        impsb = T("impsb", [128, 128], F32)
        imptmp = T("imptmp", [128, 128], F32)
        m8 = T("m8", [128, 16], F32)
        rc = T("rc", [128, 8], F32)
        den = T("den", [128, 24], F32)
        coef = T("coef", [128, 24], F32)
        oc_sb = T("ocsb", [128, 2, 260], F32)
        ctmp = T("ctmp", [128, 4, 64], F32)
        u_sb = T("u", [128, 512], F32)
        hidf = T("hidf", [128, 256], F32)
        hidt = T("hidt", [128, 256], F32)
        v_f = T("vf", [128, 8, 64], F32)
        t512 = T("t512", [128, 512], F32)
        v_bf = T("vbf", [128, 8, 64], BF16)
        st8 = T("st8", [128, 40], F32)
        gsig = [T(f"gsig{k}", [128, 24], F32) for k in range(2)]
        yraw = T("yraw", [128, 16, 64], F32)
        ss16 = T("ss16", [128, 32], F32)
        yT = T("yT", [128, 8, 128], BF16)
        rt = T("rt", [128, D], F32)
        C.ln_stats = T("lst", [128, 12], F32)
        C.ln_mv = T("lmv", [128, 4], F32)
        C.eps_t = T("eps", [128, 1], F32)
        ps = [stack.enter_context(nc.psum_tensor(f"m{l}_ps{k}", [128, 512], F32)) for k in range(8)]
        y_bf = rt.bitcast(BF16)
        pend = [None]

        def flush():
            if pend[0] is not None:
                f_ = pend[0]
                pend[0] = None
                f_()
        psb = [p.bitcast(BF16) for p in ps]
        scT = G.modT[:, l * 48 + 8:l * 48 + 16]
        shT = G.modT[:, l * 48 + 0:l * 48 + 8]
        PS_OC, PS_OS, PS_OW, PS_IMP = 2, 3, 4, 5

        gi = [0]

        def gbank():
            gi[0] ^= 1
            return 6 + gi[0]

        si = [0]

        def sbank():
            si[0] ^= 1
            return si[0]

        pi = [0]

        def nextp():
            pi[0] = (pi[0] + 1) % 3
            return pi[0]

        V = "vector"
        s.op(V, lambda e: e.memset(C.eps_t[:, :], EPS), writes=["eps"])
        s.op(V, lambda e: e.memset(kTc[0:64, :, :], 0.0), writes=["kTc"])
        s.op(V, lambda e: e.memset(vcT[:, :, :], 0.0), writes=["vcT"])
        s.op(V, lambda e: e.memset(vc[:, :, :, :], 0.0), writes=["vc"])
        s.op(V, lambda e: e.memset(vc[:, :, :, 64:65], 1.0), writes=["vc"])
        s.op(V, lambda e: e.memset(vs[:, :, :, 64:65], 1.0), writes=["vs1"])
        s.op(V, lambda e: e.memset(vw[:, :, :, 64:65], 1.0), writes=["vw1"])
        s.op(V, lambda e: e.memset(nmq[:, :], 0.0), writes=["nmq"])
        s.op(V, lambda e: e.memset(kcb[:, :, :, :], 0.0), writes=["kcb"])
        dma(s, "sync", identf[:, :], G.c_ident[:, :], [], ["identf"])
        dma(s, "sync", tril[:, :], G.c_tril[:, :], [], ["tril"])
        dma(s, "gpsimd", identb[:, :], G.c_ident[:, :], [], ["identb"])
        for z in range(2):
            dma(s, "gpsimd", mcmp[:, z * 1088:(z + 1) * 1088], G.c_mcmp[:, z * 1088:(z + 1) * 1088], [], ["mcmp"])
        dma(s, "gpsimd", mdiag[:, :], G.c_mdiag[:, :], [], ["mdiag"])
        dma(s, "gpsimd", mtail[:, :], G.c_mtail[:, :], [], ["mtail"])
        for h_ in range(2):
            for z in range(2):
                o_ = h_ * 2048 + z * 1024
                dma(s, "gpsimd", esel[:, o_:o_ + 1024], G.c_esel[:, o_:o_ + 1024], [], ["esel"])
        dma(s, "gpsimd", ovl[:, :, :], G.c_ovl[:, :, :], [], ["ovl"])
        for kvh in range(2):
            for z in range(0, S, 1024):
                dma(s, "gpsimd", kTs[64:67, kvh, z:z + 1024], G.c_kaug[:, z:z + 1024], [], ["kTs_aug"])
            dma(s, "gpsimd", kTc[64:67, kvh, :], G.c_caug[:, :], [], ["kTc_aug"])
        dma(s, "sync", ln1g[:, :], G.ln1_g[l:l + 1, :].partition_broadcast(128), [], ["lnp"])
        dma(s, "sync", ln1b[:, :], G.ln1_b[l:l + 1, :].partition_broadcast(128), [], ["lnp"])
        dma(s, "sync", vng[:, :], G.vn_g[l:l + 1, :].partition_broadcast(128), [], ["vnp"])
        dma(s, "sync", vnb[:, :], G.vn_b[l:l + 1, :].partition_broadcast(128), [], ["vnp"])
        dma(s, "sync", outgT[:, :], G.out_g[l].rearrange("(c p) -> p c", p=128), [], ["outgT"], allow_slow_non_contiguous=True)
        dma(s, "sync", bsT[:, :], G.b_s[l].rearrange("g t -> t g"), [], ["bsT"], allow_slow_non_contiguous=True)
        dma(s, "sync", bcast(rt[:, 0:1], [[128, 8], [1, 128]]), G.w_s[l].rearrange("g t s -> t g s"), [], ["rt"])
        for kv in range(2):
            dma(s, "sync", posTf[:, kv, :], G.cmp_pos[l, kv].rearrange("r d -> (r d)").rearrange("(r p) -> p r", p=128),
                [], ["posTf"], allow_slow_non_contiguous=True)
            dma(s, "gpsimd", w1c[:, kv, :, :], G.cmp_w1[l, kv].rearrange("(r p) h -> p r h", p=128), [], ["w1c"])
            dma(s, "gpsimd", w2c[:, kv, :, :], G.cmp_w2[l, kv].rearrange("(h p) d -> p h d", p=128), [], ["w2c"])
        for c in range(8):
            for h in range(2):
                dma(s, "gpsimd", winb[:, c, h * 1164:(h + 1) * 1164], G.w_in[l, c * 128:(c + 1) * 128, h * 1164:(h + 1) * 1164],
                    [], ["win"])
            dma(s, "gpsimd", wob[:, c, :], G.w_o[l, c * 128:(c + 1) * 128, :], [], ["wo"])
        vcopy(s, posT[:, :, :], posTf[:, :, :], ["posTf"], ["posT"])
        for g in range(8):
            b = gbank()
            tr(s, ps[b][:, 0:128], rt[:, g * 128:(g + 1) * 128], identf[:, :], ["rt", "identf"], [("ps", b)])
            vtt(s, wsT[:, g, :], ps[b][:, 0:128], tril[:, :], ALU.mult, [("ps", b), "tril"], ["wsT"])
        dma(s, "sync", rt[:, :], G.gb_d[l, 0, :, :], [], ["rt"])
        for c in range(8):
            vtt(s, wob[:, c, :], wob[:, c, :], rt[:, :], ALU.mult, ["wo", "rt"], ["wo"])
        b = gbank()
        for kv in range(2):
            for half in range(2):
                cidx = kv * 2 + half
                for r in range(16):
                    mm(s, ps[b][:, cidx * 2:cidx * 2 + 1], w1c[:, kv, r, half * 128:(half + 1) * 128], posT[:, kv, r:r + 1],
                       r == 0, r == 15, ["w1c", "posT"], [("ps", b)])
        vcopy(s, posb[:, :], bcast(ps[b][:, 0:1], [[2, 4]]), [("ps", b)], ["posb"])

        def exp_tile(sbk, reads_extra=()):
            k = nextp()
            act(s, pT[k][:, :], ps[sbk][:, :], AF.Exp, [("ps", sbk)], [("pT", k)])
            return k

        def s3(b):
            return bcast(ps[b][:, 0:1], [[128, 4], [1, 128]])

        later = []

        def pair(s_stage, pv_stage, npop=0):
            k = s_stage()
            flush()
            pend[0] = lambda: pv_stage(k)
            for _ in range(npop):
                if later:
                    later.pop(0)()

        def flush_later():
            while later:
                later.pop(0)()

        def make_R(i, xk):
            pieces = []

            def p_rms():
                rt3 = bcast(rt[:, 0:1], [[64, 16], [1, 64]])
                vtt(s, rt3, yraw[:, :, :], yraw[:, :, :], ALU.mult, ["yraw"], ["rt"])
                s.op(V, lambda e: e.tensor_reduce(ss16[:, 0:16], rt3, AX.X, ALU.add), reads=["rt"], writes=["ss16"])
                act(s, ss16[:, 16:32], ss16[:, 0:16], AF.Sqrt, ["ss16", "eps"], ["ss16b"], bias=C.eps_t[:, 0:1], scale=1.0 / 64)
                s.op(V, lambda e: e.reciprocal(ss16[:, 0:16], ss16[:, 16:32]), reads=["ss16b"], writes=["ss16"])
                vtt(s, bcast(y_bf[:, 0:1], [[64, 16], [1, 64]]), yraw[:, :, :], bcast(ss16[:, 0:1], [[1, 16], [0, 64]]), ALU.mult,
                    ["yraw", "ss16"], ["rt"])
            pieces.append(p_rms)

            def p_tr(c4):
                b = gbank()
                for cc in range(4):
                    c = c4 * 4 + cc
                    tr(s, psb[b][:, cc * 128:(cc + 1) * 128], y_bf[:, c * 128:(c + 1) * 128], identb[:, :],
                       ["rt", "identb"], [("ps", b)])
                for cc in range(4):
                    c = c4 * 4 + cc
                    act(s, yT[:, c, :], psb[b][:, cc * 128:(cc + 1) * 128], AF.Copy, [("ps", b), "outgT"], ["yT"],
                        scale=outgT[:, c:c + 1])
            pieces.append(lambda: p_tr(0))
            pieces.append(lambda: p_tr(1))

            def p_out(h):
                b = gbank()
                for c in range(8):
                    mm(s, ps[b][:, :], yT[:, c, :], wob[:, c, h * 512:(h + 1) * 512], c == 0, c == 7, ["yT", "wo"], [("ps", b)])
                vstt(s, rt[:, h * 512:(h + 1) * 512], xt[xk][:, 0, h * 512:(h + 1) * 512], ALPHA, ps[b][:, :], ALU.mult, ALU.add,
                     [("x", xk), ("ps", b)], ["rt"])
            pieces.append(lambda: p_out(0))
            pieces.append(lambda: p_out(1))

            def p_ln():
                layer_norm_tile(s, C, rt[:, :], rt[:, :], ln1g[:, :], ln1b[:, :], "rt", "rt", "lnp")
                dma(s, "sync", x_out[i * 128:(i + 1) * 128, :], rt[:, :], ["rt"], ["xout"])
            pieces.append(p_ln)
            return pieces

        t5 = bcast(t512[:, 0:1], [[64, 8], [1, 64]])

        def make_P(i):
            T0 = i * 128
            xk = i % 3
            slot = i % 8
            qTi = qT[i % 2]
            gs = gsig[i % 2]
            nlo = max(0, NB_ * i - 1)
            nhi = NB_ * i + NB_ - 1
            cnt = nhi - nlo
            c0 = 16 * nlo - T0 + 16
            lhs_tok = lambda c: hT[:, c, :]
            st_ = {}
            pieces = []

            def p_x(c4):
                if c4 == 0:
                    dma(s, "gpsimd", qTi[64:67, :, :, :], G.c_qaug[:, :, :, T0:T0 + ST], [], [("qT_aug", i % 2)])
                    for kvh in range(2):
                        dma(s, "gpsimd", kTw[64:67, kvh, slot * 128:slot * 128 + 128], G.c_kaug[:, T0:T0 + ST], [], [("kTw_aug", slot)])
                b = gbank()
                for cc in range(4):
                    c = c4 * 4 + cc
                    tr(s, ps[b][:, cc * 128:(cc + 1) * 128], xt[xk][:, 0, c * 128:(c + 1) * 128], identf[:, :],
                       [("x", xk), "identf"], [("ps", b)])
                for cc in range(4):
                    c = c4 * 4 + cc
                    act(s, hT[:, c, :], ps[b][:, cc * 128:(cc + 1) * 128], AF.Identity,
                        [("ps", b), "modT"], ["hT"], bias=shT[:, c:c + 1], scale=scT[:, c:c + 1])
            pieces.append(lambda: p_x(0))
            pieces.append(lambda: p_x(1))

            def p_kcvc(kvh):
                for kv in range(2):
                    b = gbank()
                    col0 = (C_KC if kv == 0 else C_VC) + 64 * kvh
                    for half in range(2):
                        for c in range(8):
                            mm(s, ps[b][half * 64:(half + 1) * 64, 0:ST], winb[:, c, col0:col0 + 64], hT[:, c, :], c == 0, c == 7,
                               ["win", "hT"], [("ps", b)])
                    vcopy(s, kcb[0:64, kv, kvh, 16:16 + ST], ps[b][0:64, 0:ST], [("ps", b)], ["kcb"])
                    act(s, kcb[64:128, kv, kvh, 15:15 + ST], ps[b][64:128, 0:ST], AF.Copy, [("ps", b)], ["kcb"])
            pieces.append(lambda: p_kcvc(0))
            pieces.append(lambda: p_kcvc(1))

            def p_c1(kv):
                if kv == 0:
                    st_["bh"] = gbank()
                bh = st_["bh"]
                for kvh in range(2):
                    for half in range(2):
                        combo = kv * 4 + kvh * 2 + half
                        for r in range(16):
                            rhs = bcast(kcb[:, kv, kvh, c0 + 2 * r:c0 + 2 * r + 1], [[16, cnt]])
                            mm(s, ps[bh][:, combo * 32:combo * 32 + cnt], w1c[:, kv, r, half * 128:(half + 1) * 128], rhs,
                               r == 0, r == 15, ["w1c", "kcb"], [("ps", bh)])
            pieces.append(lambda: p_c1(0))
            pieces.append(lambda: p_c1(1))

            def p_c1b():
                bh = st_["bh"]
                for kv in range(2):
                    for kvh in range(2):
                        for half in range(2):
                            combo = kv * 4 + kvh * 2 + half
                            vts(s, hidf[:, combo * 32:combo * 32 + 32], ps[bh][:, combo * 32:combo * 32 + 32],
                                posb[:, kv * 2 + half:kv * 2 + half + 1], None, ALU.add, None, [("ps", bh), "posb"], ["hidf"])
                gelu_tanh(s, C, hidT[:, :], hidf[:, :], hidt[:, :], "hidt", ["hidf"], ["hidT"])
                vcopy(s, kcb[:, :, :, 0:16], kcb[:, :, :, ST:ST + 16], ["kcb"], ["kcb"])
            pieces.append(p_c1b)

            def p_q(h0):
                for h in range(h0, h0 + 4):
                    b = gbank()
                    for c in range(8):
                        mm(s, ps[b][0:64, 0:ST], winb[:, c, C_Q + 64 * h:C_Q + 64 * h + 64], hT[:, c, :], c == 0, c == 7,
                           ["win", "hT"], [("ps", b)])
                    act(s, qTi[0:64, h // 4, h % 4, :], ps[b][0:64, 0:ST], AF.Identity, [("ps", b)], [("qT", i % 2)], scale=0.125)
            pieces.append(lambda: p_q(0))
            pieces.append(lambda: p_q(4))

            def p_k(kvh):
                b = gbank()
                for c in range(8):
                    mm(s, ps[b][0:64, 0:ST], winb[:, c, C_KS + 64 * kvh:C_KS + 64 * kvh + 64], hT[:, c, :], c == 0, c == 7,
                       ["win", "hT"], [("ps", b)])
                vcopy(s, kTs[0:64, kvh, T0:T0 + ST], ps[b][0:64, 0:ST], [("ps", b)], [("kTs", i)])
                b = gbank()
                for c in range(8):
                    mm(s, ps[b][0:64, 0:ST], winb[:, c, C_KW + 64 * kvh:C_KW + 64 * kvh + 64], hT[:, c, :], c == 0, c == 7,
                       ["win", "hT"], [("ps", b)])
                vcopy(s, kTw[0:64, kvh, slot * 128:slot * 128 + 128], ps[b][0:64, 0:ST], [("ps", b)], [("kTw", slot)])
            pieces.append(lambda: p_k(0))
            pieces.append(lambda: p_k(1))

            def p_c2():
                b = gbank()
                for kv in range(2):
                    for kvh in range(2):
                        o0 = (kv * 2 + kvh) * 32
                        for half in range(2):
                            combo = kv * 4 + kvh * 2 + half
                            mm(s, ps[b][0:64, o0:o0 + cnt], w2c[:, kv, half, :], hidT[:, combo * 32:combo * 32 + cnt],
                               half == 0, half == 1, ["w2c", "hidT"], [("ps", b)])
                for kvh in range(2):
                    vcopy(s, kTc[0:64, kvh, nlo:nhi], ps[b][0:64, kvh * 32:kvh * 32 + cnt], [("ps", b)], ["kTc"])
                    vcopy(s, vcT[:, kvh, nlo:nhi], ps[b][0:64, 64 + kvh * 32:64 + kvh * 32 + cnt], [("ps", b)], ["vcT"])
                for c in sorted(set([nlo // 128, (nhi - 1) // 128])):
                    b = gbank()
                    for kvh in range(2):
                        tr(s, psb[b][:, kvh * 64:(kvh + 1) * 64], vcT[:, kvh, c * 128:(c + 1) * 128], identb[0:64, 0:64],
                           ["vcT", "identb"], [("ps", b)])
                    vcopy(s, vc[:, c, :, 0:64], bcast(psb[b][:, 0:1], [[64, 2], [1, 64]]), [("ps", b)], ["vc"])
            pieces.append(p_c2)

            def p_tok3():
                b = gbank()
                for c in range(8):
                    mm(s, ps[b][:, 0:128], lhs_tok(c), winb[:, c, C_VS:C_VS + 128], c == 0, c == 7, ["win", "hT"], [("ps", b)])
                for c in range(8):
                    mm(s, ps[b][:, 128:280], lhs_tok(c), winb[:, c, C_VW:C_VW + 152], c == 0, c == 7, ["win", "hT"], [("ps", b)])
                vcopy(s, vs[:, i, :, 0:64], bcast(ps[b][:, 0:1], [[64, 2], [1, 64]]), [("ps", b)], [("vs", i)])
                vcopy(s, vw[:, i % 8, :, 0:64], bcast(ps[b][:, 128:129], [[64, 2], [1, 64]]), [("ps", b)], [("vw", i % 8)])
                act(s, gs[:, :], ps[b][:, 256:280], AF.Sigmoid, [("ps", b)], [("gsig", i % 2)])
            pieces.append(p_tok3)

            def p_tok_u():
                b = gbank()
                for c in range(8):
                    mm(s, ps[b][:, :], lhs_tok(c), winb[:, c, C_U:C_U + 512], c == 0, c == 7, ["win", "hT"], [("ps", b)])
                act(s, u_sb[:, :], ps[b][:, :], AF.Copy, [("ps", b)], ["u"])
            pieces.append(p_tok_u)

            def p_tok_v():
                b = gbank()
                for c in range(8):
                    mm(s, ps[b][:, :], lhs_tok(c), winb[:, c, C_V:C_V + 512], c == 0, c == 7, ["win", "hT"], [("ps", b)])
                act(s, v_f[:, :, :], ps[b][:, :], AF.Copy, [("ps", b)], ["vf"])
            pieces.append(p_tok_v)
            return pieces

        def p_uv():
            gelu_tanh(s, C, u_sb[:, :], u_sb[:, :], t512[:, :], "t512", ["u"], ["u"])
            gelu_tanh(s, C, v_f[:, :, :], v_f[:, :, :], t5, "t512", ["vf"], ["vf"])
            s.op(V, lambda e: e.tensor_reduce(st8[:, 0:8], v_f[:, :, :], AX.X, ALU.add), reads=["vf"], writes=["st8a"])
            vtt(s, t5, v_f[:, :, :], v_f[:, :, :], ALU.mult, ["vf"], ["t512"])
            s.op(V, lambda e: e.tensor_reduce(st8[:, 8:16], t5, AX.X, ALU.add), reads=["t512"], writes=["st8b"])
            vts(s, st8[:, 0:8], st8[:, 0:8], 1.0 / 64, None, ALU.mult, None, ["st8a"], ["st8a"])
            vtt(s, st8[:, 16:24], st8[:, 0:8], st8[:, 0:8], ALU.mult, ["st8a"], ["st8c"])
            vstt(s, st8[:, 8:16], st8[:, 8:16], 1.0 / 64, st8[:, 16:24], ALU.mult, ALU.subtract, ["st8b", "st8c"], ["st8b"])
            act(s, st8[:, 24:32], st8[:, 8:16], AF.Sqrt, ["st8b", "eps"], ["st8d"], bias=C.eps_t[:, 0:1])
            s.op(V, lambda e: e.reciprocal(st8[:, 32:40], st8[:, 24:32]), reads=["st8d"], writes=["st8e"])
            vtt(s, v_f[:, :, :], v_f[:, :, :], bcast(st8[:, 0:1], [[1, 8], [0, 64]]), ALU.subtract, ["vf", "st8a"], ["vf"])
            vtt(s, v_f[:, :, :], v_f[:, :, :], bcast(st8[:, 32:33], [[1, 8], [0, 64]]), ALU.mult, ["vf", "st8e"], ["vf"])
            vtt(s, v_f[:, :, :], v_f[:, :, :], bcast(vng[:, 0:1], [[64, 8], [1, 64]]), ALU.mult, ["vf", "vnp"], ["vf"])
            vtt(s, v_bf[:, :, :], v_f[:, :, :], bcast(vnb[:, 0:1], [[64, 8], [1, 64]]), ALU.add, ["vf", "vnp"], ["vbf"])

        def p_G():
            b = gbank()
            for g in range(8):
                mm(s, ps[b][:, g * 64:(g + 1) * 64], wsT[:, g, :], v_bf[:, g, :], True, True, ["wsT", "vbf"], [("ps", b)])
            for g in range(8):
                vstt(s, yraw[:, 8 + g, :], ps[b][:, g * 64:(g + 1) * 64], bsT[:, g:g + 1], u_sb[:, g * 64:(g + 1) * 64],
                     ALU.add, ALU.mult, [("ps", b), "bsT", "u"], ["yraw"])

        dma(s, "sync", xt[0][:, :, :], x_in[0:128, :].rearrange("(a p) d -> p a d", p=128), [], [("x", 0)])
        for p_ in make_P(0):
            p_()
        for i in range(NT):
            st = i
            a = 0
            T0 = i * 128
            xk = i % 3
            if i + 1 < NT:
                dma(s, "sync", xt[(i + 1) % 3][:, :, :], x_in[T0 + 128:T0 + 256, :].rearrange("(a p) d -> p a d", p=128),
                    [], [("x", (i + 1) % 3)])
            later.append(p_uv)
            later.append(p_G)
            if i + 1 < NT:
                later.extend(make_P(i + 1))

            need_topk = (2 * i + 2) > TOPK
            OCb, IMPb, OWb, OSb = (2, 4), (3, 5), (2, 4), (3, 5)

            def qr(kvh):
                return qT[i % 2][0:67, kvh, :, a * 128:(a + 1) * 128]

            for kvh in range(2):
                nmax = 8 * i + 6
                chunks = list(range(0, min(NCH - 1, nmax // 128) + 1))
                nmm = len(chunks) * 4
                for ci, c in enumerate(chunks):
                    def s_stage(c=c, kvh=kvh):
                        sbk = sbank()
                        partial = 128 * (c + 1) >= 8 * i
                        mm(s, s3(sbk), kTc[0:67, kvh, c * 128:(c + 1) * 128], qr(kvh), True, not partial,
                           ["kTc", "kTc_aug", ("qT", i % 2), ("qT_aug", i % 2)], [("ps", sbk)])
                        if partial:
                            dlt = 128 * i - 2048 * c
                            assert 0 <= dlt <= 2048
                            mm(s, s3(sbk), identb[:, :], bcast(mcmp[:, dlt:dlt + 1], [[0, 4], [1, 128]]), False, True,
                               ["identb", "mcmp"], [("ps", sbk)])
                        return exp_tile(sbk)

                    def pv_stage(k, c=c, kvh=kvh, ci=ci, nmm=nmm):
                        for g in range(4):
                            n_ = ci * 4 + g
                            mm(s, ps[OCb[kvh]][:, g * 65:(g + 1) * 65], pT[k][:, g * 128:(g + 1) * 128], vc[:, c, kvh, :],
                               n_ == 0, n_ == nmm - 1, [("pT", k), "vc"], [("ps", OCb[kvh])])
                            if need_topk:
                                mm(s, ps[IMPb[kvh]][:, g * 128:(g + 1) * 128], pT[k][:, g * 128:(g + 1) * 128], ovl[:, c, :],
                                   n_ == 0, n_ == nmm - 1, [("pT", k), "ovl"], [("ps", IMPb[kvh])])
                    pair(s_stage, pv_stage)
                flush()
                vcopy(s, oc_sb[:, kvh, :], ps[OCb[kvh]][:, 0:260], [("ps", OCb[kvh])], [("oc", kvh)])
                if need_topk:
                    W = 2 * i + 2
                    PI = IMPb[kvh]
                    vts(s, rc[:, 0:4], bcast(oc_sb[:, kvh, 64:65], [[65, 4]]), 1e-30, None, ALU.add, None, [("oc", kvh)], ["rc"])
                    s.op(V, lambda e: e.reciprocal(rc[:, 4:8], rc[:, 0:4]), reads=["rc"], writes=["rci"])
                    vts(s, impsb[:, 0:W], ps[PI][:, 0:W], rc[:, 4:5], None, ALU.mult, None, [("ps", PI), "rci"], ["imp"])
                    for g in range(1, 4):
                        vstt(s, impsb[:, 0:W], ps[PI][:, g * 128:g * 128 + W], rc[:, 4 + g:5 + g], impsb[:, 0:W],
                             ALU.mult, ALU.add, [("ps", PI), "rci", "imp"], ["imp"])
                    s.op(V, lambda e: e.memset(impsb[:, 0:1], 1e6), reads=["imp"], writes=["imp"])
                    s.op(V, lambda e, i=i: e.memset(impsb[0:64, 2 * i - 1:2 * i + 1], 1e6), reads=["imp"], writes=["imp"])
                    s.op(V, lambda e, i=i: e.memset(impsb[0:64, 2 * i + 1:2 * i + 2], -1e30), reads=["imp"], writes=["imp"])
                    s.op(V, lambda e, i=i: e.memset(impsb[64:128, 2 * i:2 * i + 2], 1e6), reads=["imp"], writes=["imp"])
                    s.op(V, lambda e, W=W: e.max(m8[:, 0:8], impsb[:, 0:W]), reads=["imp"], writes=["m8a"])
                    s.op(V, lambda e, W=W: e.match_replace(imptmp[:, 0:W], m8[:, 0:8], impsb[:, 0:W], -1e30),
                         reads=["imp", "m8a"], writes=["imptmp"])
                    s.op(V, lambda e, W=W: e.max(m8[:, 8:16], imptmp[:, 0:W]), reads=["imptmp"], writes=["m8b"])
                    vts(s, nmq[:, 0:W], impsb[:, 0:W], m8[:, 15:16], NEGM, ALU.is_lt, ALU.mult, ["imp", "m8b"], ["nmq"])
                    b = gbank()
                    tr(s, psb[b][:, 0:128], nmq[:, :], identb[:, :], ["nmq", "identb"], [("ps", b)])
                    vcopy(s, nmT[:, kvh, :], psb[b][:, 0:128], [("ps", b)], [("nmT", kvh)])
            for kvh in range(2):
                kts = list(range(max(0, i - 4), i + 1))
                nmm = len(kts) * 4
                for ci, kt in enumerate(kts):
                    def s_stage(kt=kt, kvh=kvh):
                        sbk = sbank()
                        rs = (kt % 8) * 128
                        extra = (kt == i) or (kt == i - 4)
                        mm(s, s3(sbk), kTw[0:67, kvh, rs:rs + 128], qr(kvh), True, not extra,
                           [("kTw", kt % 8), ("kTw_aug", kt % 8), ("qT", i % 2), ("qT_aug", i % 2)], [("ps", sbk)])
                        if kt == i:
                            mm(s, s3(sbk), identb[:, :], bcast(mdiag[:, 0:1], [[0, 4], [1, 128]]), False, True,
                               ["identb", "mdiag"], [("ps", sbk)])
                        elif kt == i - 4:
                            mm(s, s3(sbk), identb[:, :], bcast(mtail[:, 0:1], [[0, 4], [1, 128]]), False, True,
                               ["identb", "mtail"], [("ps", sbk)])
                        return exp_tile(sbk)

                    def pv_stage(k, kt=kt, kvh=kvh, ci=ci, nmm=nmm):
                        for g in range(4):
                            n_ = ci * 4 + g
                            mm(s, ps[OWb[kvh]][:, g * 65:(g + 1) * 65], pT[k][:, g * 128:(g + 1) * 128], vw[:, kt % 8, kvh, :],
                               n_ == 0, n_ == nmm - 1, [("pT", k), ("vw", kt % 8), "vw1"], [("ps", OWb[kvh])])
                    pair(s_stage, pv_stage)
            SEL_NEAR = 16
            sel_kts = [list(range(i + 1)) if kvh == 1 or i < SEL_NEAR + 1 else [0] + list(range(i - SEL_NEAR + 1, i + 1))
                       for kvh in range(2)]
            tot_sel = len(sel_kts[0]) + len(sel_kts[1])
            done_sel = 0
            for kvh in range(2):
                nmm = len(sel_kts[kvh]) * 4
                for ci, kt in enumerate(sel_kts[kvh]):
                    def s_stage(kt=kt, kvh=kvh):
                        sbk = sbank()
                        last_is_qk = (not need_topk) and kt != i
                        mm(s, s3(sbk), kTs[0:67, kvh, kt * 128:(kt + 1) * 128], qr(kvh), True, last_is_qk,
                           [("kTs", kt // NSUB), "kTs_aug", ("qT", i % 2), ("qT_aug", i % 2)], [("ps", sbk)])
                        if need_topk:
                            m_ = kt // 32
                            e0 = 128 * (kt - 32 * m_)
                            mm(s, s3(sbk), esel[64 * m_:64 * m_ + 64, e0:e0 + 128],
                               bcast(nmT[64 * m_:64 * m_ + 64, kvh, 0:1], [[0, 4], [1, 128]]), False, kt != i,
                               ["esel", ("nmT", kvh)], [("ps", sbk)])
                        if kt == i:
                            mm(s, s3(sbk), identb[:, :], bcast(mdiag[:, 0:1], [[0, 4], [1, 128]]), False, True,
                               ["identb", "mdiag"], [("ps", sbk)])
                        return exp_tile(sbk)

                    def pv_stage(k, kt=kt, kvh=kvh, nmm=nmm, ci=ci):
                        for g in range(4):
                            n_ = ci * 4 + g
                            mm(s, ps[OSb[kvh]][:, g * 65:(g + 1) * 65], pT[k][:, g * 128:(g + 1) * 128], vs[:, kt, kvh, :],
                               n_ == 0, n_ == nmm - 1, [("pT", k), ("vs", kt), "vs1"], [("ps", OSb[kvh])])
                    rem = tot_sel - done_sel
                    done_sel += 1
                    pair(s_stage, pv_stage, npop=-(-len(later) // max(1, rem - 1)))
            flush()
            flush_later()
            for kvh in range(2):
                d0 = kvh * 12
                srcs = (oc_sb[:, kvh, :], ps[OWb[kvh]][:, 0:260], ps[OSb[kvh]][:, 0:260])
                keys = (("oc", kvh), ("ps", OWb[kvh]), ("ps", OSb[kvh]))
                order = (0, 2, 1)
                for br in range(3):
                    sidx = order[br]
                    vts(s, bcast(den[:, d0 + br:d0 + br + 1], [[3, 4]]), bcast(srcs[sidx][:, 64:65], [[65, 4]]), 1e-30, None,
                        ALU.add, None, [keys[sidx]], [("den", kvh)])
                s.op(V, lambda e, d0=d0: e.reciprocal(den[:, d0:d0 + 12], den[:, d0:d0 + 12]), reads=[("den", kvh)], writes=[("den", kvh)])
                vtt(s, coef[:, d0:d0 + 12], den[:, d0:d0 + 12], gsig[i % 2][:, d0:d0 + 12], ALU.mult, [("den", kvh), ("gsig", i % 2)], [("coef", kvh)])
                yv = yraw[:, kvh * 4:(kvh + 1) * 4, :]
                cb = lambda br: bcast(coef[:, d0 + br:d0 + br + 1], [[3, 4], [0, 64]])
                v4 = lambda ap: bcast(ap[:, 0:1], [[65, 4], [1, 64]])
                vtt(s, yv, v4(srcs[0]), cb(0), ALU.mult, [keys[0], ("coef", kvh)], ["yraw"])
                vtt(s, ctmp[:, :, :], v4(srcs[2]), cb(1), ALU.mult, [keys[2], ("coef", kvh)], ["ctmp"])
                vtt(s, yv, yv, ctmp[:, :, :], ALU.add, ["yraw", "ctmp"], ["yraw"])
                vtt(s, ctmp[:, :, :], v4(srcs[1]), cb(2), ALU.mult, [keys[1], ("coef", kvh)], ["ctmp"])
                vtt(s, yv, yv, ctmp[:, :, :], ALU.add, ["yraw", "ctmp"], ["yraw"])


            later.extend(make_R(i, xk))
        flush_later()
        s.run()


_CACHE = {}


def kernel(**inputs):
    x = np.asarray(inputs["x"], dtype=np.float32)
    B, S, _ = x.shape
    shapes = {n: tuple(np.asarray(inputs[n]).shape) for n in W_NAMES}
    key = (B, S)
    if key not in _CACHE:
        _CACHE[key] = build(S, DEPTH, shapes, mode="full")
    nc = _CACHE[key]
    consts = make_consts(S)
    in_maps = []
    for b in range(B):
        m = {"x": np.ascontiguousarray(x[b]), "c": np.ascontiguousarray(np.asarray(inputs["c"], dtype=np.float32)[b])}
        for n in W_NAMES:
            m[n] = np.ascontiguousarray(np.asarray(inputs[n], dtype=np.float32))
        m.update(consts)
        in_maps.append(m)
    res = run_bass_kernel_spmd(nc, in_maps, core_ids=list(range(B)))
    return np.stack([np.asarray(r["out"], dtype=np.float32) for r in res.results], axis=0)
```
